# Optimizing a Trainium2 kernel written in Bass

```python
import jax
import jax.numpy as jnp
from jax import lax
import numpy as np

D_MODEL = 2048
BATCH = 4
SEQ = 2048
DEPTH = 4

N_MIXERS = 3
N_POOL_LAYERS = (DEPTH + 2) // 3
N_SGU_LAYERS = (DEPTH + 1) // 3
N_NSA_LAYERS = DEPTH // 3

RMS_EPS = 1e-6
LN_EPS = 1e-5
NEG = -1e30
BIG = 1e30

FFN_DIM = ((8 * D_MODEL // 3 + 255) // 256) * 256
CONV_WIDTH = 3

POOL_DIM = D_MODEL
POOL_WINDOWS = (2, 4, 8, 16)
POOL_GROUPS = len(POOL_WINDOWS)
POOL_GROUP_DIM = POOL_DIM // POOL_GROUPS

SGU_DIM = D_MODEL
SGU_CHUNK = 128
SGU_GROUPS = 16
SGU_GROUP_DIM = SGU_DIM // SGU_GROUPS

HEAD_DIM = 128
N_HEADS = D_MODEL // HEAD_DIM
N_KV_HEADS = 4
GQA_REP = N_HEADS // N_KV_HEADS
ROPE_THETA = 10000.0
CMP_BLOCK = 32
CMP_STRIDE = 16
CMP_HIDDEN = 2 * HEAD_DIM
SLC_BLOCK = 64
SLC_TOPK = 16
SLC_QBLOCK = 16
WIN = 512
WIN_QBLOCK = 128
NSA_Q_DIM = N_HEADS * HEAD_DIM
NSA_KV_DIM = N_KV_HEADS * HEAD_DIM
NSA_N_GATES = 3
NSA_IN_DIM = NSA_Q_DIM + 6 * NSA_KV_DIM + NSA_N_GATES * N_HEADS

kernel_name = 'hybrid_pool_sgu_nsa_trunk'


def rmsnorm(x, g):
    xf = x.astype(jnp.float32)
    y = xf * lax.rsqrt(jnp.mean(xf * xf, axis=-1, keepdims=True) + RMS_EPS)
    return (y * g.astype(jnp.float32)).astype(x.dtype)


def layernorm(x, g, b):
    xf = x.astype(jnp.float32)
    mu = jnp.mean(xf, axis=-1, keepdims=True)
    var = jnp.mean(jnp.square(xf - mu), axis=-1, keepdims=True)
    y = (xf - mu) * lax.rsqrt(var + LN_EPS)
    return (y * g.astype(jnp.float32) + b.astype(jnp.float32)).astype(x.dtype)


def rope(x, pos):
    half = HEAD_DIM // 2
    inv = 1.0 / (ROPE_THETA ** (jnp.arange(half, dtype=jnp.float32) / half))
    ang = pos.astype(jnp.float32)[:, None] * inv[None, :]
    cos = jnp.cos(ang)[None, :, None, :]
    sin = jnp.sin(ang)[None, :, None, :]
    x1 = x[..., :half].astype(jnp.float32)
    x2 = x[..., half:].astype(jnp.float32)
    return jnp.concatenate([x1 * cos - x2 * sin, x2 * cos + x1 * sin], axis=-1).astype(x.dtype)


def pool_mixer(h, w_in, w_grp, scale, w_out):
    B, T, _ = h.shape
    z = (h @ w_in).astype(jnp.float32)
    cs = jnp.pad(jnp.cumsum(z, axis=1), ((0, 0), (1, 0), (0, 0)))
    t = jnp.arange(T)
    outs = []
    for g, w in enumerate(POOL_WINDOWS):
        sl = slice(g * POOL_GROUP_DIM, (g + 1) * POOL_GROUP_DIM)
        start = jnp.maximum(t + 1 - w, 0)
        win_sum = cs[:, 1:, sl] - cs[:, start, sl]
        cnt = (t + 1 - start).astype(jnp.float32)[None, :, None]
        outs.append(win_sum / cnt - z[:, :, sl])
    p = jnp.stack(outs, axis=2).astype(h.dtype)
    m = jnp.einsum('btgc,gcd->btgd', p, w_grp).reshape(B, T, POOL_DIM)
    return (m * scale) @ w_out


def sgu_mixer(h, w_in, ln_g, ln_b, w_s, b_s, w_out):
    B, T, _ = h.shape
    u, v = jnp.split(jax.nn.gelu(h @ w_in), 2, axis=-1)
    v = layernorm(v, ln_g, ln_b)
    n_chunks = T // SGU_CHUNK
    v = v.reshape(B, n_chunks, SGU_CHUNK, SGU_GROUPS, SGU_GROUP_DIM)
    causal = jnp.tril(jnp.ones((SGU_CHUNK, SGU_CHUNK), dtype=bool))
    ws = jnp.where(causal[None], w_s, jnp.zeros((), w_s.dtype))
    mixed = jnp.einsum('gts,bcsgd->bctgd', ws, v) + b_s.T[None, None, :, :, None]
    return (u * mixed.reshape(B, T, SGU_DIM)) @ w_out


def compress(a, pe, w1, b1, w2, b2):
    B, T = a.shape[:2]
    nb = T // CMP_STRIDE
    r = CMP_BLOCK // CMP_STRIDE
    nc = nb - r + 1
    ab = a.reshape(B, nb, CMP_STRIDE, N_KV_HEADS, HEAD_DIM)
    win = jnp.concatenate([ab[:, j:j + nc] for j in range(r)], axis=2)
    win = win + pe[None, None, :, None, :]
    flat = jnp.swapaxes(win, 2, 3).reshape(B, nc, N_KV_HEADS, CMP_BLOCK * HEAD_DIM)
    return jax.nn.gelu(flat @ w1 + b1) @ w2 + b2


def nsa_mixer(h, w_in, gate_b, cmp_pe, cmp_w1, cmp_b1, cmp_w2, cmp_b2, w_out):
    B, T, _ = h.shape
    pos = jnp.arange(T)
    scale = HEAD_DIM ** -0.5
    splits = np.cumsum([NSA_Q_DIM] + [NSA_KV_DIM] * 6).tolist()
    q, kc, vc, ks, vs, kw, vw, g = jnp.split(h @ w_in, splits, axis=-1)
    kv_shape = (B, T, N_KV_HEADS, HEAD_DIM)
    q = rope(q.reshape(B, T, N_HEADS, HEAD_DIM), pos)
    kc = rope(kc.reshape(kv_shape), pos)
    ks = rope(ks.reshape(kv_shape), pos)
    kw = rope(kw.reshape(kv_shape), pos)
    vc, vs, vw = vc.reshape(kv_shape), vs.reshape(kv_shape), vw.reshape(kv_shape)
    gates = jax.nn.sigmoid((g + gate_b).astype(jnp.float32)).astype(h.dtype)
    gates = gates.reshape(B, T, N_KV_HEADS, GQA_REP, NSA_N_GATES)
    qg = q.reshape(B, T, N_KV_HEADS, GQA_REP, HEAD_DIM)

    kcmp = compress(kc, cmp_pe[0], cmp_w1[0], cmp_b1[0], cmp_w2[0], cmp_b2[0])
    vcmp = compress(vc, cmp_pe[1], cmp_w1[1], cmp_b1[1], cmp_w2[1], cmp_b2[1])
    nc = kcmp.shape[1]
    s_c = jnp.einsum('btgrd,bcgd->bgrtc', qg, kcmp).astype(jnp.float32) * scale
    c_end = jnp.arange(nc) * CMP_STRIDE + CMP_BLOCK - 1
    m_c = c_end[None, :] <= pos[:, None]
    s_c = jnp.where(m_c, s_c, NEG)
    p_c = jax.nn.softmax(s_c, axis=-1) * jnp.any(m_c, axis=-1)[:, None].astype(jnp.float32)
    o_c = jnp.einsum('bgrtc,bcgd->btgrd', p_c.astype(vcmp.dtype), vcmp)

    ns = T // SLC_BLOCK
    ci = np.arange(nc)[:, None]
    sj = np.arange(ns)[None, :]
    overlap = (ci * CMP_STRIDE <= (sj + 1) * SLC_BLOCK - 1) & (ci * CMP_STRIDE + CMP_BLOCK - 1 >= sj * SLC_BLOCK)
    imp = jnp.einsum('bgrtc,cs->bgts', p_c, jnp.asarray(overlap.astype(np.float32)))
    blk = jnp.arange(ns)[None, :]
    cur = (pos // SLC_BLOCK)[:, None]
    forced = (blk == 0) | (blk == cur) | (blk == cur - 1)
    imp = jnp.where(forced, BIG, imp)
    imp = jnp.where(blk <= cur, imp, NEG)
    n_sel = min(SLC_TOPK, ns)
    top_s, top_i = lax.top_k(imp, n_sel)
    sel_ok = top_s > 0.5 * NEG

    kblk = ks.reshape(B, ns, SLC_BLOCK, N_KV_HEADS, HEAD_DIM).transpose(0, 3, 1, 2, 4)
    vblk = vs.reshape(B, ns, SLC_BLOCK, N_KV_HEADS, HEAD_DIM).transpose(0, 3, 1, 2, 4)
    nq = T // SLC_QBLOCK
    qm = qg.reshape(B, nq, SLC_QBLOCK, N_KV_HEADS, GQA_REP, HEAD_DIM).transpose(1, 0, 3, 2, 4, 5)
    im = top_i.reshape(B, N_KV_HEADS, nq, SLC_QBLOCK, n_sel).transpose(2, 0, 1, 3, 4)
    okm = sel_ok.reshape(B, N_KV_HEADS, nq, SLC_QBLOCK, n_sel).transpose(2, 0, 1, 3, 4)
    pm = pos.reshape(nq, SLC_QBLOCK)
    gather = jax.vmap(jax.vmap(lambda blocks, ix: blocks[ix]))
    offs = jnp.arange(SLC_BLOCK)

    def slc_block(args):
        qb, ib, okb, pb = args
        kg = gather(kblk, ib)
        vg = gather(vblk, ib)
        s = jnp.einsum('bgtrd,bgtnld->bgtrnl', qb, kg).astype(jnp.float32) * scale
        kpos = ib[..., None] * SLC_BLOCK + offs
        m = okb[..., None] & (kpos <= pb[None, None, :, None, None])
        s = jnp.where(m[:, :, :, None], s, NEG)
        p = jax.nn.softmax(s.reshape(s.shape[:4] + (-1,)), axis=-1).reshape(s.shape)
        return jnp.einsum('bgtrnl,bgtnld->bgtrd', p.astype(vg.dtype), vg)

    o_s = lax.map(slc_block, (qm, im, okm, pm))
    o_s = o_s.transpose(1, 0, 3, 2, 4, 5).reshape(B, T, N_KV_HEADS, GQA_REP, HEAD_DIM)

    nb = T // WIN_QBLOCK
    nband = WIN // WIN_QBLOCK + 1

    def band(a):
        ap = jnp.pad(a, ((0, 0), (WIN, 0), (0, 0), (0, 0))).reshape(B, nb + nband - 1, WIN_QBLOCK, N_KV_HEADS, HEAD_DIM)
        return jnp.concatenate([ap[:, j:j + nb] for j in range(nband)], axis=2)

    kband, vband = band(kw), band(vw)
    qw = qg.reshape(B, nb, WIN_QBLOCK, N_KV_HEADS, GQA_REP, HEAD_DIM)
    s_w = jnp.einsum('bntgrd,bnkgd->bngrtk', qw, kband).astype(jnp.float32) * scale
    qpos = jnp.arange(nb)[:, None] * WIN_QBLOCK + jnp.arange(WIN_QBLOCK)[None, :]
    kpos = jnp.arange(nb)[:, None] * WIN_QBLOCK - WIN + jnp.arange(nband * WIN_QBLOCK)[None, :]
    diff = qpos[:, :, None] - kpos[:, None, :]
    m_w = (diff >= 0) & (diff < WIN) & (kpos[:, None, :] >= 0)
    s_w = jnp.where(m_w[None, :, None, None], s_w, NEG)
    p_w = jax.nn.softmax(s_w, axis=-1)
    o_w = jnp.einsum('bngrtk,bnkgd->bntgrd', p_w.astype(vband.dtype), vband)
    o_w = o_w.reshape(B, T, N_KV_HEADS, GQA_REP, HEAD_DIM)

    o = gates[..., 0:1] * o_c + gates[..., 1:2] * o_s + gates[..., 2:3] * o_w
    return o.reshape(B, T, NSA_Q_DIM) @ w_out


def conv_ffn(h, w_up, conv_w, conv_b, w_down):
    a, b = jnp.split(h @ w_up, 2, axis=-1)
    a = lax.conv_general_dilated(a, conv_w[:, None, :], window_strides=(1,),
                                 padding=[(CONV_WIDTH - 1, 0)],
                                 dimension_numbers=('NWC', 'WIO', 'NWC'),
                                 feature_group_count=FFN_DIM) + conv_b
    return (jax.nn.silu(a) * b) @ w_down


def setup_inputs(seed: int = 0) -> dict:
    key = jax.random.key(seed)
    keys = jax.random.split(key, 32)
    counter = [0]

    def nrm(shape, s):
        k = keys[counter[0]]
        counter[0] += 1
        return s * jax.random.normal(k, shape, jnp.float32)

    D = D_MODEL
    return {
        'x': nrm((BATCH, SEQ, D), 1.0),
        'norm_g': 1.0 + nrm((DEPTH, 4, D), 0.02),
        'ffn_w_up': nrm((DEPTH, D, 2 * FFN_DIM), D ** -0.5),
        'ffn_conv_w': nrm((DEPTH, CONV_WIDTH, FFN_DIM), CONV_WIDTH ** -0.5),
        'ffn_conv_b': nrm((DEPTH, FFN_DIM), 0.01),
        'ffn_w_down': nrm((DEPTH, FFN_DIM, D), FFN_DIM ** -0.5),
        'pool_w_in': nrm((N_POOL_LAYERS, D, POOL_DIM), D ** -0.5),
        'pool_w_grp': nrm((N_POOL_LAYERS, POOL_GROUPS, POOL_GROUP_DIM, POOL_GROUP_DIM), POOL_GROUP_DIM ** -0.5),
        'pool_scale': 1.0 + nrm((N_POOL_LAYERS, POOL_DIM), 0.02),
        'pool_w_out': nrm((N_POOL_LAYERS, POOL_DIM, D), POOL_DIM ** -0.5),
        'sgu_w_in': nrm((N_SGU_LAYERS, D, 2 * SGU_DIM), D ** -0.5),
        'sgu_ln_g': 1.0 + nrm((N_SGU_LAYERS, SGU_DIM), 0.02),
        'sgu_ln_b': nrm((N_SGU_LAYERS, SGU_DIM), 0.01),
        'sgu_w_s': nrm((N_SGU_LAYERS, SGU_GROUPS, SGU_CHUNK, SGU_CHUNK), SGU_CHUNK ** -0.5),
        'sgu_b_s': 1.0 + nrm((N_SGU_LAYERS, SGU_GROUPS, SGU_CHUNK), 0.02),
        'sgu_w_out': nrm((N_SGU_LAYERS, SGU_DIM, D), SGU_DIM ** -0.5),
        'nsa_w_in': nrm((N_NSA_LAYERS, D, NSA_IN_DIM), D ** -0.5),
        'nsa_gate_b': nrm((N_NSA_LAYERS, NSA_N_GATES * N_HEADS), 0.01),
        'nsa_cmp_pe': nrm((N_NSA_LAYERS, 2, CMP_BLOCK, HEAD_DIM), 0.02),
        'nsa_cmp_w1': nrm((N_NSA_LAYERS, 2, CMP_BLOCK * HEAD_DIM, CMP_HIDDEN), (CMP_BLOCK * HEAD_DIM) ** -0.5),
        'nsa_cmp_b1': nrm((N_NSA_LAYERS, 2, CMP_HIDDEN), 0.01),
        'nsa_cmp_w2': nrm((N_NSA_LAYERS, 2, CMP_HIDDEN, HEAD_DIM), CMP_HIDDEN ** -0.5),
        'nsa_cmp_b2': nrm((N_NSA_LAYERS, 2, HEAD_DIM), 0.01),
        'nsa_w_out': nrm((N_NSA_LAYERS, NSA_Q_DIM, D), NSA_Q_DIM ** -0.5),
    }


def reference(x, norm_g, ffn_w_up, ffn_conv_w, ffn_conv_b, ffn_w_down,
              pool_w_in, pool_w_grp, pool_scale, pool_w_out,
              sgu_w_in, sgu_ln_g, sgu_ln_b, sgu_w_s, sgu_b_s, sgu_w_out,
              nsa_w_in, nsa_gate_b, nsa_cmp_pe, nsa_cmp_w1, nsa_cmp_b1, nsa_cmp_w2, nsa_cmp_b2, nsa_w_out):
    h = x
    for i in range(DEPTH):
        kind, j = i % N_MIXERS, i // N_MIXERS
        u = rmsnorm(h, norm_g[i, 0])
        if kind == 0:
            m = pool_mixer(u, pool_w_in[j], pool_w_grp[j], pool_scale[j], pool_w_out[j])
        elif kind == 1:
            m = sgu_mixer(u, sgu_w_in[j], sgu_ln_g[j], sgu_ln_b[j], sgu_w_s[j], sgu_b_s[j], sgu_w_out[j])
        else:
            m = nsa_mixer(u, nsa_w_in[j], nsa_gate_b[j], nsa_cmp_pe[j], nsa_cmp_w1[j], nsa_cmp_b1[j],
                          nsa_cmp_w2[j], nsa_cmp_b2[j], nsa_w_out[j])
        h = h + rmsnorm(m, norm_g[i, 1])
        f = conv_ffn(rmsnorm(h, norm_g[i, 2]), ffn_w_up[i], ffn_conv_w[i], ffn_conv_b[i], ffn_w_down[i])
        h = h + rmsnorm(f, norm_g[i, 3])
    return h
```

```python
import contextlib
import numpy as np
import concourse.bass as bass
import concourse.mybir as mybir

F32 = mybir.dt.float32
BF16 = mybir.dt.bfloat16
AF = mybir.ActivationFunctionType
ALU = mybir.AluOpType

ENGS = ("pe", "act", "dve", "pool", "sp")
SEM_MAX = 30000
N_DMA_SEMS = 12


class Inst:
    __slots__ = ("eng", "fn", "deps", "signal", "is_dma", "sem", "val", "idx", "tag", "cc_inc")

    def __init__(self, eng, fn, is_dma=False, tag=None):
        self.eng = eng
        self.fn = fn
        self.deps = []
        self.signal = False
        self.is_dma = is_dma
        self.sem = None
        self.val = None
        self.idx = None
        self.tag = tag
        self.cc_inc = None


class Prog:
    def __init__(self, same_engine_sync=("act", "dve", "pool")):
        self.nc = bass.Bass("TRN2", target_bir_lowering=False)
        self.stack = contextlib.ExitStack()
        self.lists = {e: [] for e in ENGS}
        self.recs = {}
        self.tinfo = {}
        self.same_engine_sync = set(same_engine_sync)
        self.const = set()
        self.dma_rr = {"hw": 0, "sw": 0}
        self.dma_last = {}
        self.dma_count = {}
        self.all_dmas = []
        self.n_inst = 0

    def dram(self, name, shape, dtype, kind="Internal"):
        t = self.nc.dram_tensor(name, list(shape), dtype, kind=kind)
        self.tinfo[name] = ("dram", None, kind)
        return t.ap()

    def sbuf(self, name, shape, dtype):
        t = self.stack.enter_context(self.nc.sbuf_tensor(name, list(shape), dtype))
        pstep = int(np.prod(shape[1:])) * mybir.dt.size(dtype)
        self.tinfo[name] = ("sbuf", pstep, None)
        return t

    def psum(self, name, shape, dtype):
        t = self.stack.enter_context(self.nc.psum_tensor(name, list(shape), dtype))
        pstep = int(np.prod(shape[1:])) * mybir.dt.size(dtype)
        self.tinfo[name] = ("psum", pstep, None)
        return t

    def _box(self, ap):
        name = ap.tensor.name
        space, pstep, extra = self.tinfo[name]
        esz = mybir.dt.size(ap.dtype)
        dims = [(st * esz, c) for st, c in ap.ap]
        off = ap.offset * esz
        if space == "dram":
            lo = off
            hi = off + sum(abs(s) * (c - 1) for s, c in dims) + esz
            return name, space, (0, 1, lo, hi)
        ps, pc = dims[0]
        if ps == pstep or pc == 1:
            p0 = off // pstep
            p1 = p0 + pc
            lo = off % pstep
            rest = dims[1:]
        else:
            p0, p1 = 0, 128
            lo = off % pstep
            rest = dims
        hi = lo + sum(abs(s) * (c - 1) for s, c in rest) + esz
        if space == "psum":
            bank = 2048
            lo = (lo // bank) * bank
            hi = ((hi + bank - 1) // bank) * bank
            p0, p1 = 0, 128
        return name, space, (p0, p1, lo, hi)

    @staticmethod
    def _overlap(a, b):
        return a[0] < b[1] and b[0] < a[1] and a[2] < b[3] and b[2] < a[3]

    @staticmethod
    def _contains(a, b):
        return a[0] <= b[0] and b[1] <= a[1] and a[2] <= b[2] and b[3] <= a[3]

    def _track(self, inst, reads, writes):
        deps = {}
        for is_w, lst in ((False, reads), (True, writes)):
            for item in lst:
                if item is None:
                    continue
                if isinstance(item, tuple):
                    base, key = item
                    name = base if isinstance(base, str) else base.tensor.name
                    space = self.tinfo[name][0] if name in self.tinfo else "dram"
                    tkey = (name, key)
                    box = (0, 1, 0, 1)
                else:
                    name, space, box = self._box(item)
                    tkey = name
                if space == "dram" and name in self.tinfo and self.tinfo[name][2] == "ExternalInput":
                    continue
                mode = "w" if is_w else ("x" if space == "psum" else "r")
                recs = self.recs.setdefault(tkey, [])
                if tkey in self.const:
                    assert not is_w, f"write to const tensor {tkey}"
                    for r in recs:
                        if r[2] == "w" and self._overlap(r[0], box):
                            deps[id(r[1])] = r[1]
                    continue
                keep = []
                for r in recs:
                    rbox, rinst, rmode = r
                    if rinst is inst:
                        keep.append(r)
                        continue
                    ov = self._overlap(rbox, box)
                    if ov:
                        conflict = (mode == "w" or rmode == "w" or
                                    (mode == "x" and rmode == "x" and rinst.eng != inst.eng))
                        if conflict:
                            deps[id(rinst)] = rinst
                    if mode == "w" and self._contains(box, rbox):
                        continue
                    if (mode != "w" and rmode == mode and rinst.eng == inst.eng and rbox == box
                            and not rinst.is_dma and not inst.is_dma):
                        continue
                    keep.append(r)
                keep.append([box, inst, mode])
                self.recs[tkey] = keep
        for d in deps.values():
            if d is inst:
                continue
            if d.eng == inst.eng and not d.is_dma and not inst.is_dma and d.eng not in self.same_engine_sync:
                continue
            inst.deps.append(d)
            d.signal = True

    def mark_const(self, t):
        name = t if isinstance(t, str) else t.tensor.name
        self.const.add(name)

    def op(self, eng, fn, reads=(), writes=(), tag=None):
        inst = Inst(eng, fn, tag=tag)
        self._track(inst, list(reads), list(writes))
        self.lists[eng].append(inst)
        self.n_inst += 1
        return inst

    def dma(self, q, out, in_, reads=None, writes=None, **kw):
        def fn(e, out=out, in_=in_, kw=kw):
            return e.dma_start(out=out, in_=in_, **kw)
        inst = Inst(q, fn, is_dma=True)
        inst.signal = True
        qt = "sw" if q == "pool" else "hw"
        si = (qt, self.dma_rr[qt])
        self.dma_rr[qt] = (self.dma_rr[qt] + 1) % N_DMA_SEMS
        inst.sem = si
        self.dma_count[si] = self.dma_count.get(si, 0) + 1
        inst.val = 16 * self.dma_count[si]
        prev = self.dma_last.get(si)
        if prev is not None:
            inst.deps.append(prev)
        self.dma_last[si] = inst
        self._track(inst, [in_] if reads is None else reads, [out] if writes is None else writes)
        self.lists[q].append(inst)
        self.all_dmas.append(inst)
        self.n_inst += 1
        return inst

    def cc(self, fn, reads, writes, inc=16):
        inst = Inst("pool", fn, is_dma=True)
        inst.signal = True
        inst.cc_inc = inc
        si = ("cc", 0)
        inst.sem = si
        self.dma_count[si] = self.dma_count.get(si, 0) + 1
        inst.val = inc * self.dma_count[si]
        prev = self.dma_last.get(si)
        if prev is not None:
            inst.deps.append(prev)
        self.dma_last[si] = inst
        self._track(inst, reads, writes)
        self.lists["pool"].append(inst)
        self.n_inst += 1
        return inst

    def mm(self, out, lhsT, rhs, start=True, stop=True, **kw):
        def fn(e):
            return e.matmul(out, lhsT, rhs, start=start, stop=stop, **kw)
        return self.op("pe", fn, reads=[lhsT, rhs], writes=[out])

    def transpose(self, out, in_, ident):
        def fn(e):
            return e.transpose(out, in_, ident)
        return self.op("pe", fn, reads=[in_, ident], writes=[out])

    def act(self, out, in_, func, bias=None, scale=None, accum_out=None, eng="act"):
        kw = {}
        reads = [in_]
        if bias is not None:
            kw["bias"] = bias
            if not isinstance(bias, (int, float)):
                reads.append(bias)
        if scale is not None:
            kw["scale"] = scale
            if not isinstance(scale, (int, float)):
                reads.append(scale)
        writes = [out]
        if accum_out is not None:
            kw["accum_out"] = accum_out
            writes.append(accum_out)

        def fn(e):
            return e.activation(out=out, in_=in_, func=func, **kw)
        return self.op(eng, fn, reads=reads, writes=writes)

    def tt(self, out, in0, in1, op, eng="dve"):
        def fn(e):
            return e.tensor_tensor(out=out, in0=in0, in1=in1, op=op)
        return self.op(eng, fn, reads=[in0, in1], writes=[out])

    def ts(self, out, in0, s1, s2=None, op0=ALU.mult, op1=None, eng="dve", accum_out=None):
        reads = [in0]
        if not isinstance(s1, (int, float)):
            reads.append(s1)
        if s2 is not None and not isinstance(s2, (int, float)):
            reads.append(s2)
        kw = {}
        if op1 is not None:
            kw["op1"] = op1
        writes = [out]
        if accum_out is not None:
            kw["accum_out"] = accum_out
            writes.append(accum_out)

        def fn(e):
            return e.tensor_scalar(out=out, in0=in0, scalar1=s1, scalar2=s2, op0=op0, **kw)
        return self.op(eng, fn, reads=reads, writes=writes)

    def stt(self, out, in0, scalar, in1, op0, op1, eng="dve"):
        reads = [in0, in1]
        if not isinstance(scalar, (int, float)):
            reads.append(scalar)

        def fn(e):
            return e.scalar_tensor_tensor(out=out, in0=in0, scalar=scalar, in1=in1, op0=op0, op1=op1)
        return self.op(eng, fn, reads=reads, writes=[out])

    def copy(self, out, in_, eng="dve"):
        if eng == "act":
            return self.act(out, in_, AF.Copy)

        def fn(e):
            return e.tensor_copy(out=out, in_=in_)
        return self.op(eng, fn, reads=[in_], writes=[out])

    def memset(self, ap, val, eng="dve"):
        def fn(e):
            return e.memset(ap, val)
        return self.op(eng, fn, reads=[], writes=[ap])

    def emit(self):
        nc = self.nc
        fin = Inst("sp", None)
        for d in self.dma_last.values():
            fin.deps.append(d)
        gens = {}
        for e in ENGS:
            n = 0
            for inst in self.lists[e]:
                if inst.is_dma:
                    continue
                if inst.signal:
                    inst.sem = (e, n // SEM_MAX)
                    inst.val = n % SEM_MAX + 1
                    n += 1
            gens[e] = (n + SEM_MAX - 1) // SEM_MAX
        sems = {}
        for e in ENGS:
            for g in range(gens[e]):
                sems[(e, g)] = self.stack.enter_context(nc.semaphore(f"s_{e}_{g}"))
        for si in self.dma_count:
            sems[si] = self.stack.enter_context(nc.semaphore(f"s_dma_{si[0]}{si[1]}"))
        self.n_waits = 0
        prog = self

        def run(e, eng):
            waited = {}
            lst = prog.lists[e] + ([fin] if e == "sp" else [])
            for inst in lst:
                need = {}
                for d in inst.deps:
                    k = d.sem
                    if need.get(k, 0) < d.val:
                        need[k] = d.val
                for k, v in need.items():
                    if waited.get(k, 0) < v:
                        eng.wait_ge(sems[k], v)
                        waited[k] = v
                        prog.n_waits += 1
                if inst.fn is None:
                    continue
                r = inst.fn(eng)
                if inst.cc_inc is not None:
                    r.then_inc(sems[inst.sem], inst.cc_inc)
                elif inst.is_dma:
                    r.then_inc(sems[inst.sem], 16)
                elif inst.signal:
                    r.then_inc(sems[inst.sem], 1)

        with nc.Block() as block:
            @block.tensor
            def _(eng):
                run("pe", eng)

            @block.scalar
            def _(eng):
                run("act", eng)

            @block.vector
            def _(eng):
                run("dve", eng)

            @block.gpsimd
            def _(eng):
                run("pool", eng)

            @block.sync
            def _(eng):
                run("sp", eng)
        self.stack.close()
        return nc

D = 2048
DC = 16
FF = 5632
FC = 44
TL = 2048
TT = 512
NT = TL // TT
EPS_RMS = 1e-6
EPS_LN = 1e-5
POOL_W = (2, 4, 8, 16)

VC_G0, VC_G1, VC_G2, VC_G3 = 0, 16, 32, 48
VC_CW0, VC_CW1, VC_CW2, VC_CB = 64, 108, 152, 196
VC_X0 = 240
VC_N = 272


class Rot:
    def __init__(self, items):
        self.items = list(items)
        self.i = 0

    def next(self):
        r = self.items[self.i % len(self.items)]
        self.i += 1
        return r


class Builder:
    def __init__(self, layers, final_out=True):
        self.layers = layers
        P = self.P = Prog()
        self.xT = P.dram("xT", [D, TL], F32, kind="ExternalInput")
        self.outT = P.dram("outT", [D, TL], F32, kind="ExternalOutput")
        self.hT = P.dram("hT", [NT, DC, 128, TT], F32)
        self.cst = P.dram("cst", [128, CST_N], F32, kind="ExternalInput")
        self.w = {}
        for li in layers:
            kind = li % 3
            self.w[(li, "vec")] = P.dram(f"vec_{li}", [128, VC_N], F32, kind="ExternalInput")
            self.w[(li, "up")] = P.dram(f"ffn_w_up_{li}", [D, 2 * FF], F32, kind="ExternalInput")
            self.w[(li, "down")] = P.dram(f"ffn_w_down_{li}", [FF, D], F32, kind="ExternalInput")
            if kind == 2:
                for key, nm, shp in (("nin", "nsa_w_in", [D, 5168]), ("nout", "nsa_w_out", [D, D]),
                                     ("nw1", "nsa_cmp_w1", [2, 4096, 256]), ("nw2", "nsa_cmp_w2", [2, 256, 128]),
                                     ("nsmall", "nsa_small", [128, 80]), ("nb2v", "nsa_b2v", [128]),
                                     ("ngb", "nsa_gate_b", [48]), ("nmask8", "nsa_mask8", [128, 4096]),
                                     ("neall", "nsa_eall", [32, 2048]), ("novl", "nsa_ovl", [128, 132]),
                                     ("nrope", "nsa_rope", [128, 2, TL]), ("nmaskc", "nsa_maskc", [128, TL]),
                                     ("nselc", "nsa_selc", [128, 16, 96])):
                    self.w[(li, key)] = P.dram(f"{nm}_{li}", shp, F32, kind="ExternalInput")
            if kind == 1:
                self.w[(li, "sin")] = P.dram(f"sgu_w_in_{li}", [D, 2 * D], F32, kind="ExternalInput")
                self.w[(li, "sout")] = P.dram(f"sgu_w_out_{li}", [D, D], F32, kind="ExternalInput")
                self.w[(li, "sws")] = P.dram(f"sgu_w_s_{li}", [16, 128, 128], F32, kind="ExternalInput")
                self.w[(li, "sbs")] = P.dram(f"sgu_b_s_{li}", [16 * 128], F32, kind="ExternalInput")
            if kind == 0:
                self.w[(li, "pin")] = P.dram(f"pool_w_in_{li}", [D, D], F32, kind="ExternalInput")
                self.w[(li, "pgrp")] = P.dram(f"pool_w_grp_{li}", [4, 512, 512], F32, kind="ExternalInput")
                self.w[(li, "pout")] = P.dram(f"pool_w_out_{li}", [D, D], F32, kind="ExternalInput")
        self.A = P.sbuf("arenaA", [128, 22528], F32)
        self.B = P.sbuf("arenaB", [128, 8192], F32)
        self.wbufs = Rot([P.sbuf(f"wbuf{i}", [128, 6144], BF16) for i in range(3)])
        self.cst_sb = P.sbuf("cst_sb", [128, CST_N], F32)
        self.vec_sb = [P.sbuf(f"vec_sb{i}", [128, VC_N], F32) for i in range(2)]
        self.ones_bf = P.sbuf("ones_bf", [128, 128], BF16)
        self.eps_rms = P.sbuf("eps_rms", [128, 1], F32)
        self.S = P.sbuf("arenaS", [128, 4096], F32)
        S = self.S
        self.abufs = Rot([S[:, i * 514:(i + 1) * 514] for i in range(3)])
        self.accs = Rot([S[:, 1544 + i * 512:1544 + (i + 1) * 512] for i in range(2)])
        self.sils = Rot([S[:, 2568 + i * 512:2568 + (i + 1) * 512] for i in range(2)])
        self.gtmp = Rot([S[:, i * 512:(i + 1) * 512] for i in range(6)])
        self.eps_ln = P.sbuf("eps_ln", [128, 1], F32)
        self.lnst = P.sbuf("lnst", [128, 8], F32)
        self.junk = P.sbuf("junk", [128, 2048], BF16)
        self.sqs = Rot([P.sbuf(f"sq{i}", [128, 512], BF16) for i in range(3)])
        self.sd = P.sbuf("sd", [128, 512], F32)
        self.rstd = P.sbuf("rstd", [128, 512], F32)
        self.hch = Rot([P.sbuf(f"hch{i}", [128, 512], F32) for i in range(3)])
        self.ahalo = P.sbuf("ahalo", [128, FC, 2], F32)
        self.zhalo = P.sbuf("zhalo", [128, DC, 16], F32)
        self.zbufs = Rot([S[:, i * 528:(i + 1) * 528] for i in range(3)])
        self.swbufs = Rot([S[:, 1584 + i * 528:1584 + (i + 1) * 528] for i in range(4)])
        self.tmp16 = P.sbuf("tmp16", [128, 16], F32)
        self.nsm = P.sbuf("nsa_sm", [128, 2304], F32)
        self.acc_ctr = 0
        self.nsmall_sb = P.sbuf("nsmall_sb", [128, 80], F32)
        self.banks = [P.psum(f"ps{i}", [128, 512], F32) for i in range(8)]
        self.mmb = Rot(self.banks[0:4])
        self.stat = self.banks[4]
        self.misc = Rot(self.banks[5:8])

    def build(self):
        P = self.P
        P.dma("sp", self.cst_sb[:], self.cst)
        P.memset(self.ones_bf[:], 1.0)
        P.memset(self.eps_rms[:], EPS_RMS)
        P.memset(self.eps_ln[:], EPS_LN)
        for it in range(NT):
            P.dma("sp", self.hT[it].rearrange("c p t -> p c t"),
                  self.xT[:, it * TT:(it + 1) * TT].rearrange("(c p) t -> p c t", p=128))
        nl = len(self.layers)
        for idx, li in enumerate(self.layers):
            vec = self.vec_sb[idx % 2]
            P.dma("sp", vec[:], self.w[(li, "vec")])
            self.vec = vec
            kind = li % 3
            parts = getattr(self, "parts", "all")
            self.mixer_to_out = (parts == "mixer")
            if parts in ("all", "mixer"):
                if kind == 0:
                    self.pool_mixer(li)
                elif kind == 1:
                    self.sgu_mixer(li)
                else:
                    P.dma("sp", self.nsmall_sb[:], self.w[(li, "nsmall")])
                    self.nsa_mixer(li)
            if parts in ("all", "ffn"):
                self.ffn(li, last=(idx == nl - 1))
        return P.emit()

    def h_load_tile(self, it, dst):
        self.P.dma("sp", dst, self.hT[it].rearrange("c p t -> p c t"))

    def rstd_from_stat(self, n_feat, eps_tile):
        P = self.P
        P.act(self.sd[:], self.stat[:], AF.Sqrt, bias=eps_tile[:], scale=1.0 / n_feat)
        P.op("dve", lambda e: e.reciprocal(out=self.rstd[:], in_=self.sd[:]),
             reads=[self.sd[:]], writes=[self.rstd[:]])

    def prenorm(self, hin, gcol, out):
        P = self.P
        for c in range(DC):
            sq = self.sqs.next()
            P.tt(sq[:], hin[:, c, :], hin[:, c, :], ALU.mult)
            P.mm(self.stat[:], self.ones_bf[:], sq[:], start=(c == 0), stop=(c == DC - 1))
        self.rstd_from_stat(D, self.eps_rms)
        for c in range(DC):
            P.stt(out[:, c, :], hin[:, c, :], self.vec[:, gcol + c:gcol + c + 1], self.rstd[:],
                  ALU.mult, ALU.mult)

    def f_evac(self, pf, fT, n):
        P = self.P
        P.act(fT[:, n, :], pf[:], AF.Copy)
        sq = self.sqs.next()
        P.tt(sq[:], fT[:, n, :], fT[:, n, :], ALU.mult)
        P.mm(self.stat[:], self.ones_bf[:], sq[:], start=(n == 0), stop=(n == DC - 1))

    def postnorm_residual(self, fT, gcol, it, to_out):
        P = self.P
        self.rstd_from_stat(D, self.eps_rms)
        for n in range(DC):
            hc = self.hch.next()
            P.dma("sp", hc[:], self.hT[it, n])
            P.stt(fT[:, n, :], fT[:, n, :], self.vec[:, gcol + n:gcol + n + 1], self.rstd[:],
                  ALU.mult, ALU.mult)
            P.tt(hc[:], hc[:], fT[:, n, :], ALU.add)
            if to_out:
                dst = self.outT[n * 128:(n + 1) * 128, it * TT:(it + 1) * TT]
                P.dma("sp", dst, hc[:], writes=[(dst, (it, n))])
            else:
                P.dma("sp", self.hT[it, n], hc[:])

    def slab(self, wdram, n0, ncols, kc):
        wb = self.wbufs.next()
        v = wb[:, 0:kc * ncols].rearrange("p (c n) -> p c n", n=ncols)
        self.P.dma("pool", v, wdram[:, n0:n0 + ncols].rearrange("(c p) n -> p c n", p=128))
        return v

    def ffn(self, li, last):
        P = self.P
        Abf = self.A[:].bitcast(BF16)
        gT = Abf[:, 0:FC * 1024].rearrange("p (c t) -> p c t", t=1024)
        hin = self.A[:, 0:8192].rearrange("p (c t) -> p c t", t=TT)
        uT = self.B[:].bitcast(BF16).rearrange("p (c t) -> p c t", t=1024)
        fT = self.B[:].rearrange("p (c t) -> p c t", t=TT)
        w_up, w_down = self.w[(li, "up")], self.w[(li, "down")]
        vec = self.vec
        for st in range(TL // 1024):
            for half in range(2):
                it = st * 2 + half
                self.h_load_tile(it, hin)
                self.prenorm(hin, VC_G2, uT[:, :, half * TT:(half + 1) * TT])
            for j in range(FC):
                wb = self.wbufs.next()
                wv = wb[:, 0:4096].rearrange("p (h c n) -> p h c n", h=2, n=128)
                P.dma("pool", wv[:, 0], w_up[:, j * 128:(j + 1) * 128].rearrange("(c p) n -> p c n", p=128))
                P.dma("pool", wv[:, 1], w_up[:, FF + j * 128:FF + (j + 1) * 128].rearrange("(c p) n -> p c n", p=128))
                prev_ab = None
                for half in range(2):
                    ts_ = slice(half * TT, (half + 1) * TT)
                    pa, pb = self.mmb.next(), self.mmb.next()
                    for k in range(DC):
                        P.mm(pa[:], wv[:, 0, k, :], uT[:, k, ts_], start=(k == 0), stop=(k == DC - 1))
                    for k in range(DC):
                        P.mm(pb[:], wv[:, 1, k, :], uT[:, k, ts_], start=(k == 0), stop=(k == DC - 1))
                    ab = self.abufs.next()
                    if half == 0:
                        if st == 0:
                            P.memset(ab[:, 0:2], 0.0)
                        else:
                            P.copy(ab[:, 0:2], self.ahalo[:, j, :])
                    else:
                        P.copy(ab[:, 0:2], prev_ab[:, 512:514])
                    P.act(ab[:, 2:514], pa[:], AF.Copy)
                    if half == 1 and st + 1 < TL // 1024:
                        P.copy(self.ahalo[:, j, :], ab[:, 512:514])
                    prev_ab = ab
                    acc = self.accs.next()
                    P.ts(acc[:], ab[:, 2:514], vec[:, VC_CW2 + j:VC_CW2 + j + 1], vec[:, VC_CB + j:VC_CB + j + 1],
                         op0=ALU.mult, op1=ALU.add)
                    P.stt(acc[:], ab[:, 1:513], vec[:, VC_CW1 + j:VC_CW1 + j + 1], acc[:], ALU.mult, ALU.add)
                    P.stt(acc[:], ab[:, 0:512], vec[:, VC_CW0 + j:VC_CW0 + j + 1], acc[:], ALU.mult, ALU.add)
                    sil = self.sils.next()
                    P.act(sil[:], acc[:], AF.Silu)
                    P.tt(gT[:, j, ts_], sil[:], pb[:], ALU.mult)
            for half in range(2):
                it = st * 2 + half
                ts_ = slice(half * TT, (half + 1) * TT)
                for n in range(DC):
                    wv = self.slab(w_down, n * 128, 128, FC)
                    pf = self.mmb.next()
                    for k in range(FC):
                        P.mm(pf[:], wv[:, k, :], gT[:, k, ts_], start=(k == 0), stop=(k == FC - 1))
                    self.f_evac(pf, fT, n)
                self.postnorm_residual(fT, VC_G3, it, to_out=last)

    def pool_mixer(self, li):
        P = self.P
        Abf = self.A[:].bitcast(BF16)
        hin = self.A[:, 0:8192].rearrange("p (c t) -> p c t", t=TT)
        pT = Abf[:, 16384:24576].rearrange("p (c t) -> p c t", t=TT)
        msT = Abf[:, 24576:32768].rearrange("p (c t) -> p c t", t=TT)
        uT = Abf[:, 32768:40960].rearrange("p (c t) -> p c t", t=TT)
        fT = self.B[:].rearrange("p (c t) -> p c t", t=TT)
        w_in, w_grp, w_out = self.w[(li, "pin")], self.w[(li, "pgrp")], self.w[(li, "pout")]
        vec = self.vec
        invc = self.cst_sb[:, CST_INVC:CST_INVC + 16]
        self.h_load_tile(0, hin)
        self.prenorm(hin, VC_G0, uT)
        for it in range(NT):
            for n in range(DC):
                wv = self.slab(w_in, n * 128, 128, DC)
                pz = self.mmb.next()
                for k in range(DC):
                    P.mm(pz[:], wv[:, k, :], uT[:, k, :], start=(k == 0), stop=(k == DC - 1))
                w = POOL_W[n // 4]
                zb = self.zbufs.next()
                if it == 0:
                    P.memset(zb[:, 0:16], 0.0)
                else:
                    P.copy(zb[:, 0:16], self.zhalo[:, n, :])
                P.act(zb[:, 16:528], pz[:], AF.Copy)
                if it + 1 < NT:
                    P.copy(self.zhalo[:, n, :], zb[:, 512:528])
                s = zb
                sh = 1
                while sh < w:
                    rem = w - 2 * sh
                    s2 = self.swbufs.next()
                    lo = 16 - rem
                    P.tt(s2[:, lo:528], s[:, lo:528], s[:, lo - sh:528 - sh], ALU.add)
                    s = s2
                    sh *= 2
                P.stt(pT[:, n, :], s[:, 16:528], 1.0 / w, zb[:, 16:528], ALU.mult, ALU.subtract)
                if it == 0:
                    P.tt(self.tmp16[:, 0:w - 1], s[:, 16:16 + w - 1], invc[:, 0:w - 1], ALU.mult)
                    P.tt(pT[:, n, 0:w - 1], self.tmp16[:, 0:w - 1], zb[:, 16:16 + w - 1], ALU.subtract)
            for g in range(4):
                for n2 in range(4):
                    wv = self.slab(w_grp[g], n2 * 128, 128, 4)
                    pm = self.mmb.next()
                    for k in range(4):
                        P.mm(pm[:], wv[:, k, :], pT[:, g * 4 + k, :], start=(k == 0), stop=(k == 3))
                    n = g * 4 + n2
                    P.act(msT[:, n, :], pm[:], AF.Copy, scale=vec[:, VC_X0 + n:VC_X0 + n + 1])
            if it + 1 < NT:
                self.h_load_tile(it + 1, hin)
                self.prenorm(hin, VC_G0, uT)
            for n in range(DC):
                wv = self.slab(w_out, n * 128, 128, DC)
                pf = self.mmb.next()
                for k in range(DC):
                    P.mm(pf[:], wv[:, k, :], msT[:, k, :], start=(k == 0), stop=(k == DC - 1))
                self.f_evac(pf, fT, n)
            self.postnorm_residual(fT, VC_G1, it, to_out=self.mixer_to_out)


    def gelu(self, dst, src, n):
        self.P.act(dst, src, AF.Gelu_apprx_tanh)

    def sgu_mixer(self, li):
        P = self.P
        Abf = self.A[:].bitcast(BF16)
        hin = self.A[:, 0:8192].rearrange("p (c t) -> p c t", t=TT)
        uT = Abf[:, 16384:24576].rearrange("p (c t) -> p c t", t=TT)
        uG = Abf[:, 24576:32768].rearrange("p (c t) -> p c t", t=TT)
        vn = Abf[:, 32768:40960].rearrange("p (b f) -> p b f", f=D)
        gatedT = uG
        Cb = self.A[:, 20480:22528].rearrange("p (g t) -> p g t", t=128)
        wsT = self.S[:, 3072:4096].bitcast(BF16).rearrange("p (g t) -> p g t", t=128)
        vtok = self.B[:].rearrange("p (b f) -> p b f", f=D)
        fT = self.B[:].rearrange("p (c t) -> p c t", t=TT)
        w_in, w_out = self.w[(li, "sin")], self.w[(li, "sout")]
        vec = self.vec
        ident = self.cst_sb[:, CST_IDENT:CST_IDENT + 128]
        tril = self.cst_sb[:, CST_TRIL:CST_TRIL + 128]
        wst = self.B[:, 0:2048].rearrange("p (g s) -> p g s", s=128)
        P.dma("sp", wst, self.w[(li, "sws")].rearrange("g t s -> t g s"))
        bsb = self.B[:, 2048:4096].rearrange("p (g t) -> p g t", t=128)
        P.dma("sp", bsb, self.w[(li, "sbs")].partition_broadcast(128).rearrange("p (g t) -> p g t", t=128))
        for g in range(16):
            P.tt(wst[:, g, :], wst[:, g, :], tril, ALU.mult)
            pt = self.misc.next()
            P.transpose(pt[:, 0:128], wst[:, g, :], ident)
            P.copy(wsT[:, g, :], pt[:, 0:128])
        for g in range(16):
            pr = self.misc.next()
            P.mm(pr[:, 0:128], self.ones_bf[:], wsT[:, g, :], start=True, stop=True)
            P.stt(Cb[:, g, :], pr[:, 0:128], vec[:, VC_X0 + 16 + g:VC_X0 + 17 + g], bsb[:, g, :], ALU.mult, ALU.add)
        self.h_load_tile(0, hin)
        self.prenorm(hin, VC_G0, uT)
        for it in range(NT):
            for n in range(DC):
                wv = self.slab(w_in, n * 128, 128, DC)
                pu = self.mmb.next()
                for k in range(DC):
                    P.mm(pu[:], wv[:, k, :], uT[:, k, :], start=(k == 0), stop=(k == DC - 1))
                self.gelu(uG[:, n, :], pu[:], TT)
            for fs in range(8):
                wv = self.slab(w_in, D + fs * 256, 256, DC)
                for tb in range(4):
                    pv = self.mmb.next()
                    for k in range(DC):
                        P.mm(pv[:, 0:256], uT[:, k, tb * 128:(tb + 1) * 128], wv[:, k, :],
                             start=(k == 0), stop=(k == DC - 1))
                    self.gelu(vtok[:, tb, fs * 256:(fs + 1) * 256], pv[:, 0:256], 256)
            st = self.lnst
            for tb in range(4):
                P.act(self.junk[:], vtok[:, tb, :], AF.Copy, accum_out=st[:, 0:1])
                P.act(self.junk[:], vtok[:, tb, :], AF.Square, accum_out=st[:, 1:2])
                P.ts(st[:, 2:3], st[:, 0:1], 1.0 / D, None, op0=ALU.mult)
                P.tt(st[:, 3:4], st[:, 2:3], st[:, 2:3], ALU.mult)
                P.stt(st[:, 4:5], st[:, 1:2], 1.0 / D, st[:, 3:4], ALU.mult, ALU.subtract)
                P.act(st[:, 5:6], st[:, 4:5], AF.Sqrt, bias=self.eps_ln[:], scale=1.0)
                P.op("dve", lambda e, st=st: e.reciprocal(out=st[:, 6:7], in_=st[:, 5:6]),
                     reads=[st[:, 5:6]], writes=[st[:, 6:7]])
                P.stt(st[:, 7:8], st[:, 2:3], -1.0, st[:, 6:7], ALU.mult, ALU.mult)
                P.ts(vn[:, tb, :], vtok[:, tb, :], st[:, 6:7], st[:, 7:8], op0=ALU.mult, op1=ALU.add)
            for g in range(16):
                pm = self.mmb.next()
                for tb in range(4):
                    P.mm(pm[:, tb * 128:(tb + 1) * 128], vn[:, tb, g * 128:(g + 1) * 128], wsT[:, g, :],
                         start=True, stop=True)
                tmp = self.gtmp.next()
                cb4 = Cb[:, g:g + 1, :].broadcast_to([128, 4, 128])
                P.stt(tmp.rearrange("p (b t) -> p b t", t=128), pm[:].rearrange("p (b t) -> p b t", t=128),
                      vec[:, VC_X0 + g:VC_X0 + g + 1], cb4, ALU.mult, ALU.add)
                P.tt(gatedT[:, g, :], tmp, uG[:, g, :], ALU.mult)
            if it + 1 < NT:
                self.h_load_tile(it + 1, hin)
                self.prenorm(hin, VC_G0, uT)
            for n in range(DC):
                wv = self.slab(w_out, n * 128, 128, DC)
                pf = self.mmb.next()
                for k in range(DC):
                    P.mm(pf[:], wv[:, k, :], gatedT[:, k, :], start=(k == 0), stop=(k == DC - 1))
                self.f_evac(pf, fT, n)
            self.postnorm_residual(fT, VC_G1, it, to_out=self.mixer_to_out)

    def nsa_views(self):
        Abf = self.A[:].bitcast(BF16)
        v = {}
        v["ksT"] = Abf[:, 0:8192].rearrange("p (g t) -> p g t", t=TL)
        v["kwT"] = Abf[:, 8192:16384].rearrange("p (g t) -> p g t", t=TL)
        v["Vs"] = Abf[:, 16384:24640].rearrange("p (j g d) -> p j g d", g=4, d=129)
        v["Vw"] = Abf[:, 24640:32896].rearrange("p (j g d) -> p j g d", g=4, d=129)
        v["kcmpT"] = Abf[:, 32896:33408].rearrange("p (g c) -> p g c", c=128)
        v["vcx"] = Abf[:, 33408:34056].rearrange("p (g d) -> p g d", d=162)
        v["hidV"] = Abf[:, 34080:35104].rearrange("p (h g c) -> p h g c", h=2, c=128)
        v["gates"] = self.A[:, 17552:18320].rearrange("p (j c) -> p j c", c=48)
        v["uT"] = Abf[:, 36640:44832].rearrange("p (c t) -> p c t", t=TT)
        Bbf = self.B[:].bitcast(BF16)
        v["hin"] = self.B[:].rearrange("p (c t) -> p c t", t=TT)
        v["fT"] = v["hin"]
        v["qT"] = Bbf[:, 0:8192].rearrange("p (c t) -> p c t", t=TT)
        v["kc_t"] = Bbf[:, 0:2112].rearrange("p (g t) -> p g t", t=528)
        v["vc_t"] = Bbf[:, 2112:4224].rearrange("p (g t) -> p g t", t=528)
        v["hidK"] = Bbf[:, 4224:4480].rearrange("p (h c) -> p h c", c=128)
        v["ptile"] = Rot([Bbf[:, 8192 + i * 512:8192 + (i + 1) * 512] for i in range(4)])
        v["o_acc"] = self.B[:, 5120:7168].rearrange("p (b f) -> p b f", f=512)
        v["o_bf"] = Bbf[:, 14336:16384].rearrange("p (b f) -> p b f", f=512)
        v["ropex"] = Rot([self.B[:, 4096 + i * 512:4096 + (i + 1) * 512] for i in range(2)])
        v["ropet"] = Rot([self.B[:, 5120 + i * 512:5120 + (i + 1) * 512] for i in range(2)])
        v["cs"] = self.B[:, 6144:7168].rearrange("p (a t) -> p a t", t=TT)
        Sbf = self.S[:].bitcast(BF16)
        v["masks8"] = Sbf[:, 0:4096].rearrange("p (o t) -> p o t", t=TT)
        v["Eall"] = Sbf[:, 4096:6144]
        v["selbT"] = Sbf[:, 6144:6656]
        v["impacc"] = self.S[:, 3328:3456].rearrange("p (b s) -> p b s", s=32)
        v["maskc"] = Sbf[:, 6912:7424]
        v["selc"] = self.S[:, 3712:4096].rearrange("p (b s) -> p b s", s=96)
        N = self.nsm
        Nbf = N[:].bitcast(BF16)
        v["w2k"] = Nbf[:, 0:256].rearrange("p (h d) -> p h d", d=128)
        v["w2v"] = Nbf[:, 256:512].rearrange("p (h d) -> p h d", d=128)
        v["peT"] = Nbf[:, 512:576].rearrange("p (k l) -> p k l", l=32)
        v["kch"] = Nbf[:, 576:640].rearrange("p (g t) -> p g t", t=16)
        v["vch"] = Nbf[:, 640:704].rearrange("p (g t) -> p g t", t=16)
        v["ident_bf"] = Nbf[:, 704:832]
        v["bias1"] = N[:, 416:420]
        v["b2v_b"] = N[:, 420:548]
        v["gateb_b"] = N[:, 548:596]
        v["sc"] = N[:, 596:724]
        v["selwork"] = N[:, 724:884]
        return v

    def rope_store(self, v, pp, dest, first_c=0):
        P = self.P
        x = v["ropex"].next()
        P.act(x, pp[:], AF.Copy)
        pr = self.mmb.next()
        P.mm(pr[:], self.cst_sb[:, CST_RT:CST_RT + 128], x, start=True, stop=True)
        t1 = v["ropet"].next()
        P.tt(t1, x, v["cs"][:, 0, :], ALU.mult)
        P.tt(x, pr[:], v["cs"][:, 1, :], ALU.mult)
        P.tt(dest, t1, x, ALU.add)

    def nsa_mixer(self, li):
        P = self.P
        v = self.nsa_views()
        self.v = v
        w_in, w_out = self.w[(li, "nin")], self.w[(li, "nout")]
        vec = self.vec
        small = self.w[(li, "nsmall")]
        SCALE = 128.0 ** -0.5
        NEGB = -30000.0
        P.dma("pool", v["masks8"], self.w[(li, "nmask8")].rearrange("p (o t) -> p o t", t=TT))
        P.dma("pool", v["Eall"][0:32, :], self.w[(li, "neall")])
        P.dma("pool", v["peT"], small[:, 0:64].rearrange("p (k l) -> p k l", l=32))
        P.dma("pool", v["w2k"], self.w[(li, "nw2")][0].rearrange("(c p) n -> p c n", p=128))
        P.dma("pool", v["w2v"], self.w[(li, "nw2")][1].rearrange("(c p) n -> p c n", p=128))
        P.dma("pool", v["ident_bf"], self.cst[:, CST_IDENT:CST_IDENT + 128])
        P.dma("sp", v["b2v_b"], self.w[(li, "nb2v")].partition_broadcast(128))
        P.dma("sp", v["gateb_b"], self.w[(li, "ngb")].partition_broadcast(128))
        P.dma("pool", v["vcx"][:, :, 128:161], self.w[(li, "novl")].rearrange("p (g d) -> p g d", d=33))
        for j in range(16):
            P.memset(v["Vs"][:, j, :, 128:129], 1.0)
            P.memset(v["Vw"][:, j, :, 128:129], 1.0)
        for kv in range(2):
            for hc in range(2):
                wv = self.slab_w1(li, kv, hc)
                pp = self.misc.next()
                for l in range(32):
                    P.mm(pp[:, 0:1], wv[:, l, :], v["peT"][:, kv, l:l + 1], start=(l == 0), stop=(l == 31))
                P.tt(v["bias1"][:, kv * 2 + hc:kv * 2 + hc + 1], pp[:, 0:1],
                     self.nsmall_sb[:, 64 + kv * 2 + hc:65 + kv * 2 + hc], ALU.add)
        for it in range(NT):
            self.h_load_tile(it, v["hin"])
            self.prenorm(v["hin"], VC_G0, v["uT"])
            P.dma("sp", v["cs"], self.w[(li, "nrope")][:, :, it * TT:(it + 1) * TT])
            for name, col0, rope in (("kc", 2048, True), ("vc", 2560, False), ("ks", 3072, True), ("kw", 4096, True)):
                for g in range(4):
                    wv = self.slab(w_in, col0 + g * 128, 128, DC)
                    pp = self.mmb.next()
                    for k in range(DC):
                        P.mm(pp[:], wv[:, k, :], v["uT"][:, k, :], start=(k == 0), stop=(k == DC - 1))
                    if name == "kc":
                        dest = v["kc_t"][:, g, 16:528]
                    elif name == "vc":
                        dest = v["vc_t"][:, g, 16:528]
                    elif name == "ks":
                        dest = v["ksT"][:, g, it * TT:(it + 1) * TT]
                    else:
                        dest = v["kwT"][:, g, it * TT:(it + 1) * TT]
                    if rope:
                        self.rope_store(v, pp, dest)
                    else:
                        P.act(dest, pp[:], AF.Copy)
            for src, halo in ((v["kc_t"], v["kch"]), (v["vc_t"], v["vch"])):
                if it == 0:
                    P.memset(src[:, :, 0:16], 0.0)
                else:
                    P.copy(src[:, :, 0:16], halo)
                if it + 1 < NT:
                    P.copy(halo, src[:, :, 512:528])
            for V, col0 in ((v["Vs"], 3584), (v["Vw"], 4608)):
                for half in range(2):
                    wv = self.slab(w_in, col0 + half * 256, 256, DC)
                    for tb in range(4):
                        pv = self.mmb.next()
                        for k in range(DC):
                            P.mm(pv[:, 0:256], v["uT"][:, k, tb * 128:(tb + 1) * 128], wv[:, k, :],
                                 start=(k == 0), stop=(k == DC - 1))
                        P.act(V[:, it * 4 + tb, half * 2:half * 2 + 2, 0:128],
                              pv[:, 0:256].rearrange("p (g d) -> p g d", d=128), AF.Copy)
            wv = self.slab(w_in, 5120, 48, DC)
            for tb in range(4):
                pg = self.misc.next()
                for k in range(DC):
                    P.mm(pg[:, 0:48], v["uT"][:, k, tb * 128:(tb + 1) * 128], wv[:, k, :],
                         start=(k == 0), stop=(k == DC - 1))
                P.tt(v["sc"][:, 0:48], pg[:, 0:48], v["gateb_b"], ALU.add)
                P.act(v["gates"][:, it * 4 + tb, :], v["sc"][:, 0:48], AF.Sigmoid)
            j0 = 1 if it == 0 else 0
            nj = 32 - j0
            c0 = 32 * it - 1 + j0
            if it == 0:
                P.memset(v["hidK"], 0.0)
            for kv, src in ((0, v["kc_t"]), (1, v["vc_t"])):
                for hc in range(2):
                    wv = self.slab_w1(li, kv, hc)
                    ph = self.mmb.next()
                    for l in range(32):
                        rhs = src[:, :, l:l + 497:16]
                        P.mm(ph[:, 0:128].rearrange("p (g j) -> p g j", j=32), wv[:, l, :], rhs,
                             start=(l == 0), stop=(l == 31))
                    phv = ph[:, 0:128].rearrange("p (g j) -> p g j", j=32)[:, :, j0:32]
                    if kv == 0:
                        dst = v["hidK"][:, hc, :].rearrange("p (g j) -> p g j", j=32)[:, :, j0:32]
                    else:
                        dst = v["hidV"][:, hc, :, c0:c0 + nj]
                    self.gelu3(dst, phv, v["bias1"][:, kv * 2 + hc:kv * 2 + hc + 1], nj)
                if kv == 0:
                    pk = self.mmb.next()
                    for hc in range(2):
                        P.mm(pk[:, 0:128], v["w2k"][:, hc, :], v["hidK"][:, hc, :], start=(hc == 0), stop=(hc == 1))
                    P.act(v["kcmpT"][:, :, c0:c0 + nj],
                          pk[:, 0:128].rearrange("p (g j) -> p g j", j=32)[:, :, j0:32],
                          AF.Identity, bias=self.nsmall_sb[:, 68:69], scale=1.0)
        for g in range(4):
            pv = self.misc.next()
            for hc in range(2):
                P.mm(pv[0:127, 0:128], v["hidV"][:, hc, g, 0:127], v["w2v"][:, hc, :], start=(hc == 0), stop=(hc == 1))
            P.tt(v["vcx"][0:127, g, 0:128], pv[0:127, 0:128], v["b2v_b"][0:127, :], ALU.add)
        accX, accY = self.banks[5], self.banks[6]
        misc2 = Rot([self.banks[7], self.banks[4]])
        for it in range(NT):
            self.h_load_tile(it, v["hin"])
            self.prenorm(v["hin"], VC_G0, v["uT"])
            P.dma("sp", v["cs"], self.w[(li, "nrope")][:, :, it * TT:(it + 1) * TT])
            for h in range(16):
                wv = self.slab(w_in, h * 128, 128, DC)
                pq = self.mmb.next()
                for k in range(DC):
                    P.mm(pq[:], wv[:, k, :], v["uT"][:, k, :], start=(k == 0), stop=(k == DC - 1))
                self.rope_store(v, pq, v["qT"][:, h, :])
            P.dma("pool", v["maskc"], self.w[(li, "nmaskc")][:, it * TT:(it + 1) * TT])
            P.dma("sp", v["selc"], self.w[(li, "nselc")][:, it * 4:(it + 1) * 4, :])
            oT = v["uT"]
            for g in range(4):
                self.nsa_group(li, it, g, v, accX, accY, misc2, SCALE, oT)
            fT = v["fT"]
            for n in range(DC):
                wv = self.slab(w_out, n * 128, 128, DC)
                pf = self.mmb.next()
                for k in range(DC):
                    P.mm(pf[:], wv[:, k, :], oT[:, k, :], start=(k == 0), stop=(k == DC - 1))
                self.f_evac(pf, fT, n)
            self.postnorm_residual(fT, VC_G1, it, to_out=self.mixer_to_out)

    def slab_w1(self, li, kv, hc):
        wb = self.wbufs.next()
        vv = wb[:, 0:4096].rearrange("p (l n) -> p l n", n=128)
        self.P.dma("pool", vv, self.w[(li, "nw1")][kv][:, hc * 128:(hc + 1) * 128].rearrange("(l d) n -> d l n", d=128))
        return vv

    def gelu3(self, dst, src, bias, nj):
        self.P.act(dst, src, AF.Gelu_apprx_tanh, bias=bias, scale=1.0)

    def nsa_group(self, li, it, g, v, accX, accY, misc2, SCALE, oT):
        P = self.P
        qT, gates = v["qT"], v["gates"]
        ident_bf = v["ident_bf"]
        o_acc, o_bf = v["o_acc"], v["o_bf"]
        N = self.nsm
        t4s = Rot([N[:, 1024 + 8 * i:1032 + 8 * i] for i in range(8)])
        cmb = Rot([N[:, 1088 + 256 * i:1344 + 256 * i].rearrange("p (b d) -> p b d", d=128) for i in range(2)])
        tmpi = N[:, 1600:1728].rearrange("p (b s) -> p b s", s=32)
        imp2 = N[:, 1728:1856].rearrange("p (b s) -> p b s", s=32)
        imp3 = N[:, 1856:1984].rearrange("p (b s) -> p b s", s=32)
        selb = N[:, 1984:2112].rearrange("p (b s) -> p b s", s=32)
        m8 = N[:, 2112:2144].rearrange("p (b k) -> p b k", k=8)
        m8b = N[:, 2144:2176].rearrange("p (b k) -> p b k", k=8)
        selbias = N[:, 2176:2240].bitcast(BF16).rearrange("p (b s) -> p b s", s=32)
        gq = gates[:, it * 4:(it + 1) * 4, :]
        def cmp_A(r):
            h = 4 * g + r
            psc = self.mmb.next()
            P.mm(psc[0:127, :], v["kcmpT"][:, g, 0:127], qT[:, h, :], start=True, stop=False)
            P.mm(psc[0:127, :], ident_bf[0:127, 0:127], v["maskc"][0:127, :], start=False, stop=True)
            pcT = v["ptile"].next()
            P.act(pcT[0:127, :], psc[0:127, :], AF.Exp, scale=SCALE)
            return pcT

        def cmp_B(r, pcT):
            h = 4 * g + r
            pocA = self.mmb.next()
            pocB = self.mmb.next()
            for tb in range(4):
                P.mm(pocA[:, tb * 128:(tb + 1) * 128], pcT[0:127, tb * 128:(tb + 1) * 128], v["vcx"][0:127, g, 0:128],
                     start=(tb == 0), stop=False, skip_group_check=True)
            for tb in range(4):
                P.mm(pocB[:, tb * 33:(tb + 1) * 33], pcT[0:127, tb * 128:(tb + 1) * 128], v["vcx"][0:127, g, 128:161],
                     start=(tb == 0), stop=False, skip_group_check=True)
            pBv = pocB[:, 0:132].rearrange("p (b s) -> p b s", s=33)
            t4 = t4s.next()
            rs4 = t4[:, 0:4]
            P.ts(rs4, pBv[:, :, 0], 1e-30, None, op0=ALU.add)
            P.op("dve", lambda e, rs4=rs4: e.reciprocal(out=rs4, in_=rs4), reads=[rs4], writes=[rs4])
            rsb = rs4.unsqueeze(2).broadcast_to([128, 4, 32])
            if r == 0:
                P.tt(v["impacc"], pBv[:, :, 1:33], rsb, ALU.mult)
            else:
                P.tt(tmpi, pBv[:, :, 1:33], rsb, ALU.mult)
                P.tt(v["impacc"], v["impacc"], tmpi, ALU.add)
            wc4 = t4[:, 4:8]
            P.tt(wc4, rs4, gq[:, :, h * 3], ALU.mult)
            P.tt(o_acc[:, :, r * 128:(r + 1) * 128], pocA[:].rearrange("p (b d) -> p b d", d=128),
                 wc4.unsqueeze(2).broadcast_to([128, 4, 128]), ALU.mult)

        pc_cur = cmp_A(0)
        for r in range(4):
            pc_next = cmp_A(r + 1) if r < 3 else None
            cmp_B(r, pc_cur)
            pc_cur = pc_next
        P.tt(imp2, v["impacc"], v["selc"][:, :, 0:32], ALU.mult)
        P.tt(imp2, imp2, v["selc"][:, :, 32:64], ALU.add)
        for tb in range(4):
            a2, a3, b8, c8 = imp2[:, tb, :], imp3[:, tb, :], m8[:, tb, :], m8b[:, tb, :]
            P.op("dve", lambda e, b8=b8, a2=a2: e.max(out=b8, in_=a2), reads=[a2], writes=[b8])
            P.op("dve", lambda e, b8=b8, a2=a2, a3=a3: e.match_replace(out=a3, in_to_replace=b8, in_values=a2, imm_value=-1e9),
                 reads=[a2, b8], writes=[a3])
            P.op("dve", lambda e, c8=c8, a3=a3: e.max(out=c8, in_=a3), reads=[a3], writes=[c8])
        P.tt(selb, imp2, m8b[:, :, 7:8].broadcast_to([128, 4, 32]), ALU.is_ge)
        P.tt(selb, selb, v["selc"][:, :, 64:96], ALU.mult)
        P.ts(selbias, selb, 30000.0, -30000.0, op0=ALU.mult, op1=ALU.add)
        pt = self.mmb.next()
        ptb = pt[:].bitcast(BF16)
        for tb in range(4):
            P.transpose(ptb[0:32, tb * 128:(tb + 1) * 128], selbias[:, tb, :], ident_bf)
        P.copy(v["selbT"][0:32, :], ptb[0:32, 0:512])
        for br in range(2):
            KT = v["ksT"] if br == 0 else v["kwT"]
            V = v["Vs"] if br == 0 else v["Vw"]
            kt_lo = 0 if br == 0 else max(0, 4 * it - 4)
            for r in range(4):
                h = 4 * g + r
                self.acc_ctr += 1
                bX, bY = (self.banks[5], self.banks[6]) if self.acc_ctr % 2 == 0 else (self.banks[7], self.banks[4])
                started = {0: False, 1: False}
                last_kt = 4 * it + 3
                def stA(kt):
                    o = kt - 4 * it
                    if br == 0:
                        c0, c1 = max(o, 0) * 128, 512
                        tbs = list(range(max(o, 0), 4))
                    else:
                        tbs = [tb for tb in range(4) if 0 <= tb - o <= 4]
                        c0, c1 = tbs[0] * 128, (tbs[-1] + 1) * 128
                    pss = self.mmb.next()
                    need_mask = (o >= 0) if br == 0 else True
                    P.mm(pss[:, c0:c1], KT[:, g, kt * 128:(kt + 1) * 128], qT[:, h, c0:c1], start=True, stop=False)
                    if br == 0:
                        P.mm(pss[:, c0:c1], v["Eall"][0:32, kt * 128:(kt + 1) * 128], v["selbT"][0:32, c0:c1],
                             start=False, stop=(not need_mask))
                    if need_mask:
                        P.mm(pss[:, c0:c1], ident_bf, v["masks8"][:, o + 4, c0:c1], start=False, stop=True)
                    pT = v["ptile"].next()
                    P.act(pT[:, c0:c1], pss[:, c0:c1], AF.Exp, scale=SCALE)
                    return pT, tbs

                def stB(kt, pT, tbs):
                    for tb in tbs:
                        bank, bi = (bX, 0) if tb < 2 else (bY, 1)
                        av = bank[:, (tb % 2) * 129:(tb % 2) * 129 + 129]
                        P.mm(av, pT[:, tb * 128:(tb + 1) * 128], V[:, kt, g, 0:129],
                             start=(not started[bi]), stop=False, skip_group_check=True)
                        started[bi] = True

                kts = list(range(kt_lo, last_kt + 1))
                cur = stA(kts[0])
                for ki, kt in enumerate(kts):
                    nxt = stA(kts[ki + 1]) if ki + 1 < len(kts) else None
                    stB(kt, *cur)
                    cur = nxt
                t4 = t4s.next()
                rs4, w4 = t4[:, 0:4], t4[:, 4:8]
                for bi, bank in ((0, bX), (1, bY)):
                    bv = bank[:, 0:258].rearrange("p (b d) -> p b d", d=129)
                    P.op("dve", lambda e, o_=rs4[:, 2 * bi:2 * bi + 2], i_=bv[:, :, 128]: e.reciprocal(out=o_, in_=i_),
                         reads=[bv[:, :, 128]], writes=[rs4[:, 2 * bi:2 * bi + 2]])
                P.tt(w4, rs4, gq[:, :, h * 3 + 1 + br], ALU.mult)
                for bi, bank in ((0, bX), (1, bY)):
                    bv = bank[:, 0:258].rearrange("p (b d) -> p b d", d=129)
                    tmp = cmb.next()
                    P.tt(tmp, bv[:, :, 0:128], w4[:, 2 * bi:2 * bi + 2].unsqueeze(2).broadcast_to([128, 2, 128]), ALU.mult)
                    src = o_acc[:, 2 * bi:2 * bi + 2, r * 128:(r + 1) * 128]
                    if br == 0:
                        P.tt(src, src, tmp, ALU.add)
                    else:
                        P.tt(o_bf[:, 2 * bi:2 * bi + 2, r * 128:(r + 1) * 128], src, tmp, ALU.add)
        for tb in range(4):
            pt = self.mmb.next()
            ptb = pt[:].bitcast(BF16)
            for r in range(4):
                P.transpose(ptb[:, r * 128:(r + 1) * 128], o_bf[:, tb, r * 128:(r + 1) * 128], ident_bf)
            P.copy(oT[:, 4 * g:4 * g + 4, tb * 128:(tb + 1) * 128],
                   ptb[:, 0:512].rearrange("p (r t) -> p r t", t=128))


CST_INVC = 0
CST_IDENT = 16
CST_TRIL = 144
CST_RT = 272
CST_N = 400


def make_consts():
    c = np.zeros((128, CST_N), np.float32)
    c[:, CST_INVC:CST_INVC + 16] = (1.0 / np.arange(1, 17, dtype=np.float32))[None, :]
    c[:, CST_IDENT:CST_IDENT + 128] = np.eye(128, dtype=np.float32)
    c[:, CST_TRIL:CST_TRIL + 128] = np.tril(np.ones((128, 128), np.float32))
    R = np.zeros((128, 128), np.float32)
    for mm_ in range(64):
        R[mm_, mm_ + 64] = -1.0
        R[mm_ + 64, mm_] = 1.0
    c[:, CST_RT:CST_RT + 128] = R.T
    return c


def pack_cols(v):
    return np.ascontiguousarray(np.asarray(v, np.float32).reshape(-1, 128).T)


def make_inputs(inputs, layers, b):
    m = {"xT": np.ascontiguousarray(inputs["x"][b].T), "cst": make_consts()}
    for li in layers:
        kind, j = li % 3, li // 3
        vec = np.zeros((128, VC_N), np.float32)
        for q in range(4):
            vec[:, q * 16:(q + 1) * 16] = pack_cols(inputs["norm_g"][li, q])
        for q in range(3):
            vec[:, VC_CW0 + q * 44:VC_CW0 + (q + 1) * 44] = pack_cols(inputs["ffn_conv_w"][li, q])
        vec[:, VC_CB:VC_CB + 44] = pack_cols(inputs["ffn_conv_b"][li])
        m[f"ffn_w_up_{li}"] = inputs["ffn_w_up"][li]
        m[f"ffn_w_down_{li}"] = inputs["ffn_w_down"][li]
        if kind == 2:
            m[f"nsa_w_in_{li}"] = inputs["nsa_w_in"][j]
            m[f"nsa_w_out_{li}"] = inputs["nsa_w_out"][j]
            m[f"nsa_cmp_w1_{li}"] = inputs["nsa_cmp_w1"][j]
            m[f"nsa_cmp_w2_{li}"] = inputs["nsa_cmp_w2"][j]
            sm = np.zeros((128, 80), np.float32)
            sm[:, 0:32] = inputs["nsa_cmp_pe"][j, 0].T
            sm[:, 32:64] = inputs["nsa_cmp_pe"][j, 1].T
            for kv in range(2):
                sm[:, 64 + kv * 2:66 + kv * 2] = pack_cols(inputs["nsa_cmp_b1"][j, kv])
            sm[:, 68] = inputs["nsa_cmp_b2"][j, 0]
            m[f"nsa_small_{li}"] = sm
            m[f"nsa_b2v_{li}"] = np.ascontiguousarray(inputs["nsa_cmp_b2"][j, 1])
            m[f"nsa_gate_b_{li}"] = np.ascontiguousarray(inputs["nsa_gate_b"][j])
            m.update({f"{k}_{li}": val for k, val in nsa_tables().items()})
        if kind == 1:
            vec[:, VC_X0:VC_X0 + 16] = pack_cols(inputs["sgu_ln_g"][j])
            vec[:, VC_X0 + 16:VC_X0 + 32] = pack_cols(inputs["sgu_ln_b"][j])
            m[f"sgu_w_in_{li}"] = inputs["sgu_w_in"][j]
            m[f"sgu_w_out_{li}"] = inputs["sgu_w_out"][j]
            m[f"sgu_w_s_{li}"] = inputs["sgu_w_s"][j]
            m[f"sgu_b_s_{li}"] = np.ascontiguousarray(inputs["sgu_b_s"][j].reshape(-1))
        if kind == 0:
            vec[:, VC_X0:VC_X0 + 16] = pack_cols(inputs["pool_scale"][j])
            m[f"pool_w_in_{li}"] = inputs["pool_w_in"][j]
            m[f"pool_w_grp_{li}"] = inputs["pool_w_grp"][j]
            m[f"pool_w_out_{li}"] = inputs["pool_w_out"][j]
        m[f"vec_{li}"] = vec
    return m


_NSA_TABLES = None


def nsa_tables():
    global _NSA_TABLES
    if _NSA_TABLES is not None:
        return _NSA_TABLES
    NEGB = -30000.0
    s = np.arange(128)[:, None]
    t = np.arange(512)[None, :]
    m8 = np.zeros((128, 8, 512), np.float32)
    for o in range(-4, 4):
        if o >= 0:
            ok = (o * 128 + s) <= t
        else:
            ok = (t - s) < (512 + o * 128)
        m8[:, o + 4, :] = np.where(ok, 0.0, NEGB)
    eall = (np.arange(2048)[None, :] // 64 == np.arange(32)[:, None]).astype(np.float32)
    c = np.arange(128)[:, None]
    sj = np.arange(32)[None, :]
    ovl = ((c * 16 <= (sj + 1) * 64 - 1) & (c * 16 + 31 >= sj * 64)).astype(np.float32)
    ovl33 = np.concatenate([np.ones((128, 1), np.float32), ovl], axis=1)
    ovl_all = np.tile(ovl33[:, None, :], (1, 4, 1)).reshape(128, 132)
    half = 64
    inv = 1.0 / (10000.0 ** (np.arange(half, dtype=np.float32) / half))
    ang = np.arange(TL, dtype=np.float32)[None, :] * np.concatenate([inv, inv])[:, None].astype(np.float32)
    rope = np.stack([np.cos(ang), np.sin(ang)], axis=1).astype(np.float32)
    tt = np.arange(TL)[None, :]
    maskc = np.where((c * 16 + 31) <= tt, 0.0, NEGB).astype(np.float32)
    tpos = np.arange(TL)
    cur = (tpos // 64)[:, None]
    blk = np.arange(32)[None, :]
    forced = (blk == 0) | (blk == cur) | (blk == cur - 1)
    valid = blk <= cur
    keep = (valid & ~forced).astype(np.float32)
    add = np.where(forced, 10.0, 0.0) + np.where(valid, 0.0, -10.0)
    selc = np.concatenate([keep, add.astype(np.float32), valid.astype(np.float32)], axis=1)
    selc = np.ascontiguousarray(selc.reshape(16, 128, 96).transpose(1, 0, 2))
    _NSA_TABLES = {"nsa_mask8": np.ascontiguousarray(m8.reshape(128, 4096)), "nsa_eall": eall,
                   "nsa_ovl": ovl_all, "nsa_rope": rope, "nsa_maskc": maskc, "nsa_selc": selc}
    return _NSA_TABLES


from concourse.bass_utils import run_bass_kernel_spmd

N_SEQ_CORES = 4


def kernel(**inputs):
    layers = [0, 1, 2, 3]
    bld = Builder(layers)
    nc = bld.build()
    in_maps = [make_inputs(inputs, layers, b) for b in range(N_SEQ_CORES)]
    res = run_bass_kernel_spmd(nc, in_maps, core_ids=list(range(N_SEQ_CORES)))
    out = np.stack([np.asarray(res.results[b]["outT"]).T for b in range(N_SEQ_CORES)])
    return np.ascontiguousarray(out.astype(np.float32))
```

```python
import contextlib
import numpy as np
import concourse.bass as bass
import concourse.mybir as mybir

F32 = mybir.dt.float32
BF16 = mybir.dt.bfloat16
AF = mybir.ActivationFunctionType
ALU = mybir.AluOpType

ENGS = ("pe", "act", "dve", "pool", "sp")
SEM_MAX = 30000
N_DMA_SEMS = 12


class Inst:
    __slots__ = ("eng", "fn", "deps", "signal", "is_dma", "sem", "val", "idx", "tag", "cc_inc")

    def __init__(self, eng, fn, is_dma=False, tag=None):
        self.eng = eng
        self.fn = fn
        self.deps = []
        self.signal = False
        self.is_dma = is_dma
        self.sem = None
        self.val = None
        self.idx = None
        self.tag = tag
        self.cc_inc = None


class Prog:
    def __init__(self, same_engine_sync=("act", "dve", "pool")):
        self.nc = bass.Bass("TRN2", target_bir_lowering=False)
        self.stack = contextlib.ExitStack()
        self.lists = {e: [] for e in ENGS}
        self.recs = {}
        self.tinfo = {}
        self.same_engine_sync = set(same_engine_sync)
        self.const = set()
        self.dma_rr = {"hw": 0, "sw": 0}
        self.dma_last = {}
        self.dma_count = {}
        self.all_dmas = []
        self.n_inst = 0

    def dram(self, name, shape, dtype, kind="Internal"):
        t = self.nc.dram_tensor(name, list(shape), dtype, kind=kind)
        self.tinfo[name] = ("dram", None, kind)
        return t.ap()

    def sbuf(self, name, shape, dtype):
        t = self.stack.enter_context(self.nc.sbuf_tensor(name, list(shape), dtype))
        pstep = int(np.prod(shape[1:])) * mybir.dt.size(dtype)
        self.tinfo[name] = ("sbuf", pstep, None)
        return t

    def psum(self, name, shape, dtype):
        t = self.stack.enter_context(self.nc.psum_tensor(name, list(shape), dtype))
        pstep = int(np.prod(shape[1:])) * mybir.dt.size(dtype)
        self.tinfo[name] = ("psum", pstep, None)
        return t

    def _box(self, ap):
        name = ap.tensor.name
        space, pstep, extra = self.tinfo[name]
        esz = mybir.dt.size(ap.dtype)
        dims = [(st * esz, c) for st, c in ap.ap]
        off = ap.offset * esz
        if space == "dram":
            lo = off
            hi = off + sum(abs(s) * (c - 1) for s, c in dims) + esz
            return name, space, (0, 1, lo, hi)
        ps, pc = dims[0]
        if ps == pstep or pc == 1:
            p0 = off // pstep
            p1 = p0 + pc
            lo = off % pstep
            rest = dims[1:]
        else:
            p0, p1 = 0, 128
            lo = off % pstep
            rest = dims
        hi = lo + sum(abs(s) * (c - 1) for s, c in rest) + esz
        if space == "psum":
            bank = 2048
            lo = (lo // bank) * bank
            hi = ((hi + bank - 1) // bank) * bank
            p0, p1 = 0, 128
        return name, space, (p0, p1, lo, hi)

    @staticmethod
    def _overlap(a, b):
        return a[0] < b[1] and b[0] < a[1] and a[2] < b[3] and b[2] < a[3]

    @staticmethod
    def _contains(a, b):
        return a[0] <= b[0] and b[1] <= a[1] and a[2] <= b[2] and b[3] <= a[3]

    def _track(self, inst, reads, writes):
        deps = {}
        for is_w, lst in ((False, reads), (True, writes)):
            for item in lst:
                if item is None:
                    continue
                if isinstance(item, tuple):
                    base, key = item
                    name = base if isinstance(base, str) else base.tensor.name
                    space = self.tinfo[name][0] if name in self.tinfo else "dram"
                    tkey = (name, key)
                    box = (0, 1, 0, 1)
                else:
                    name, space, box = self._box(item)
                    tkey = name
                if space == "dram" and name in self.tinfo and self.tinfo[name][2] == "ExternalInput":
                    continue
                mode = "w" if is_w else ("x" if space == "psum" else "r")
                recs = self.recs.setdefault(tkey, [])
                if tkey in self.const:
                    assert not is_w, f"write to const tensor {tkey}"
                    for r in recs:
                        if r[2] == "w" and self._overlap(r[0], box):
                            deps[id(r[1])] = r[1]
                    continue
                keep = []
                for r in recs:
                    rbox, rinst, rmode = r
                    if rinst is inst:
                        keep.append(r)
                        continue
                    ov = self._overlap(rbox, box)
                    if ov:
                        conflict = (mode == "w" or rmode == "w" or
                                    (mode == "x" and rmode == "x" and rinst.eng != inst.eng))
                        if conflict:
                            deps[id(rinst)] = rinst
                    if mode == "w" and self._contains(box, rbox):
                        continue
                    if (mode != "w" and rmode == mode and rinst.eng == inst.eng and rbox == box
                            and not rinst.is_dma and not inst.is_dma):
                        continue
                    keep.append(r)
                keep.append([box, inst, mode])
                self.recs[tkey] = keep
        for d in deps.values():
            if d is inst:
                continue
            if d.eng == inst.eng and not d.is_dma and not inst.is_dma and d.eng not in self.same_engine_sync:
                continue
            inst.deps.append(d)
            d.signal = True

    def mark_const(self, t):
        name = t if isinstance(t, str) else t.tensor.name
        self.const.add(name)

    def op(self, eng, fn, reads=(), writes=(), tag=None):
        inst = Inst(eng, fn, tag=tag)
        self._track(inst, list(reads), list(writes))
        self.lists[eng].append(inst)
        self.n_inst += 1
        return inst

    def dma(self, q, out, in_, reads=None, writes=None, **kw):
        def fn(e, out=out, in_=in_, kw=kw):
            return e.dma_start(out=out, in_=in_, **kw)
        inst = Inst(q, fn, is_dma=True)
        inst.signal = True
        qt = "sw" if q == "pool" else "hw"
        si = (qt, self.dma_rr[qt])
        self.dma_rr[qt] = (self.dma_rr[qt] + 1) % N_DMA_SEMS
        inst.sem = si
        self.dma_count[si] = self.dma_count.get(si, 0) + 1
        inst.val = 16 * self.dma_count[si]
        prev = self.dma_last.get(si)
        if prev is not None:
            inst.deps.append(prev)
        self.dma_last[si] = inst
        self._track(inst, [in_] if reads is None else reads, [out] if writes is None else writes)
        self.lists[q].append(inst)
        self.all_dmas.append(inst)
        self.n_inst += 1
        return inst

    def cc(self, fn, reads, writes, inc=16):
        inst = Inst("pool", fn, is_dma=True)
        inst.signal = True
        inst.cc_inc = inc
        si = ("cc", 0)
        inst.sem = si
        self.dma_count[si] = self.dma_count.get(si, 0) + 1
        inst.val = inc * self.dma_count[si]
        prev = self.dma_last.get(si)
        if prev is not None:
            inst.deps.append(prev)
        self.dma_last[si] = inst
        self._track(inst, reads, writes)
        self.lists["pool"].append(inst)
        self.n_inst += 1
        return inst

    def mm(self, out, lhsT, rhs, start=True, stop=True, **kw):
        def fn(e):
            return e.matmul(out, lhsT, rhs, start=start, stop=stop, **kw)
        return self.op("pe", fn, reads=[lhsT, rhs], writes=[out])

    def transpose(self, out, in_, ident):
        def fn(e):
            return e.transpose(out, in_, ident)
        return self.op("pe", fn, reads=[in_, ident], writes=[out])

    def act(self, out, in_, func, bias=None, scale=None, accum_out=None, eng="act"):
        kw = {}
        reads = [in_]
        if bias is not None:
            kw["bias"] = bias
            if not isinstance(bias, (int, float)):
                reads.append(bias)
        if scale is not None:
            kw["scale"] = scale
            if not isinstance(scale, (int, float)):
                reads.append(scale)
        writes = [out]
        if accum_out is not None:
            kw["accum_out"] = accum_out
            writes.append(accum_out)

        def fn(e):
            return e.activation(out=out, in_=in_, func=func, **kw)
        return self.op(eng, fn, reads=reads, writes=writes)

    def tt(self, out, in0, in1, op, eng="dve"):
        def fn(e):
            return e.tensor_tensor(out=out, in0=in0, in1=in1, op=op)
        return self.op(eng, fn, reads=[in0, in1], writes=[out])

    def ts(self, out, in0, s1, s2=None, op0=ALU.mult, op1=None, eng="dve", accum_out=None):
        reads = [in0]
        if not isinstance(s1, (int, float)):
            reads.append(s1)
        if s2 is not None and not isinstance(s2, (int, float)):
            reads.append(s2)
        kw = {}
        if op1 is not None:
            kw["op1"] = op1
        writes = [out]
        if accum_out is not None:
            kw["accum_out"] = accum_out
            writes.append(accum_out)

        def fn(e):
            return e.tensor_scalar(out=out, in0=in0, scalar1=s1, scalar2=s2, op0=op0, **kw)
        return self.op(eng, fn, reads=reads, writes=writes)

    def stt(self, out, in0, scalar, in1, op0, op1, eng="dve"):
        reads = [in0, in1]
        if not isinstance(scalar, (int, float)):
            reads.append(scalar)

        def fn(e):
            return e.scalar_tensor_tensor(out=out, in0=in0, scalar=scalar, in1=in1, op0=op0, op1=op1)
        return self.op(eng, fn, reads=reads, writes=[out])

    def copy(self, out, in_, eng="dve"):
        if eng == "act":
            return self.act(out, in_, AF.Copy)

        def fn(e):
            return e.tensor_copy(out=out, in_=in_)
        return self.op(eng, fn, reads=[in_], writes=[out])

    def memset(self, ap, val, eng="dve"):
        def fn(e):
            return e.memset(ap, val)
        return self.op(eng, fn, reads=[], writes=[ap])

    def emit(self):
        nc = self.nc
        fin = Inst("sp", None)
        for d in self.dma_last.values():
            fin.deps.append(d)
        gens = {}
        for e in ENGS:
            n = 0
            for inst in self.lists[e]:
                if inst.is_dma:
                    continue
                if inst.signal:
                    inst.sem = (e, n // SEM_MAX)
                    inst.val = n % SEM_MAX + 1
                    n += 1
            gens[e] = (n + SEM_MAX - 1) // SEM_MAX
        sems = {}
        for e in ENGS:
            for g in range(gens[e]):
                sems[(e, g)] = self.stack.enter_context(nc.semaphore(f"s_{e}_{g}"))
        for si in self.dma_count:
            sems[si] = self.stack.enter_context(nc.semaphore(f"s_dma_{si[0]}{si[1]}"))
        self.n_waits = 0
        prog = self

        def run(e, eng):
            waited = {}
            lst = prog.lists[e] + ([fin] if e == "sp" else [])
            for inst in lst:
                need = {}
                for d in inst.deps:
                    k = d.sem
                    if need.get(k, 0) < d.val:
                        need[k] = d.val
                for k, v in need.items():
                    if waited.get(k, 0) < v:
                        eng.wait_ge(sems[k], v)
                        waited[k] = v
                        prog.n_waits += 1
                if inst.fn is None:
                    continue
                r = inst.fn(eng)
                if inst.cc_inc is not None:
                    r.then_inc(sems[inst.sem], inst.cc_inc)
                elif inst.is_dma:
                    r.then_inc(sems[inst.sem], 16)
                elif inst.signal:
                    r.then_inc(sems[inst.sem], 1)

        with nc.Block() as block:
            @block.tensor
            def _(eng):
                run("pe", eng)

            @block.scalar
            def _(eng):
                run("act", eng)

            @block.vector
            def _(eng):
                run("dve", eng)

            @block.gpsimd
            def _(eng):
                run("pool", eng)

            @block.sync
            def _(eng):
                run("sp", eng)
        self.stack.close()
        return nc

D = 2048
DC = 16
FF = 5632
FC = 44
TL = 2048
TT = 512
NT = TL // TT
EPS_RMS = 1e-6
EPS_LN = 1e-5
POOL_W = (2, 4, 8, 16)

VC_G0, VC_G1, VC_G2, VC_G3 = 0, 16, 32, 48
VC_CW0, VC_CW1, VC_CW2, VC_CB = 64, 108, 152, 196
VC_X0 = 240
VC_N = 272


class Rot:
    def __init__(self, items):
        self.items = list(items)
        self.i = 0

    def next(self):
        r = self.items[self.i % len(self.items)]
        self.i += 1
        return r


class Builder:
    def __init__(self, layers, final_out=True):
        self.layers = layers
        P = self.P = Prog()
        self.xT = P.dram("xT", [D, TL], F32, kind="ExternalInput")
        self.outT = P.dram("outT", [D, TL], F32, kind="ExternalOutput")
        self.hT = P.dram("hT", [NT, DC, 128, TT], F32)
        self.cst = P.dram("cst", [128, CST_N], F32, kind="ExternalInput")
        self.w = {}
        for li in layers:
            kind = li % 3
            self.w[(li, "vec")] = P.dram(f"vec_{li}", [128, VC_N], F32, kind="ExternalInput")
            self.w[(li, "up")] = P.dram(f"ffn_w_up_{li}", [D, 2 * FF], F32, kind="ExternalInput")
            self.w[(li, "down")] = P.dram(f"ffn_w_down_{li}", [FF, D], F32, kind="ExternalInput")
            if kind == 2:
                for key, nm, shp in (("nin", "nsa_w_in", [D, 5168]), ("nout", "nsa_w_out", [D, D]),
                                     ("nw1", "nsa_cmp_w1", [2, 4096, 256]), ("nw2", "nsa_cmp_w2", [2, 256, 128]),
                                     ("nsmall", "nsa_small", [128, 80]), ("nb2v", "nsa_b2v", [128]),
                                     ("ngb", "nsa_gate_b", [48]), ("nmask8", "nsa_mask8", [128, 4096]),
                                     ("neall", "nsa_eall", [32, 2048]), ("novl", "nsa_ovl", [128, 132]),
                                     ("nrope", "nsa_rope", [128, 2, TL]), ("nmaskc", "nsa_maskc", [128, TL]),
                                     ("nselc", "nsa_selc", [128, 16, 96])):
                    self.w[(li, key)] = P.dram(f"{nm}_{li}", shp, F32, kind="ExternalInput")
            if kind == 1:
                self.w[(li, "sin")] = P.dram(f"sgu_w_in_{li}", [D, 2 * D], F32, kind="ExternalInput")
                self.w[(li, "sout")] = P.dram(f"sgu_w_out_{li}", [D, D], F32, kind="ExternalInput")
                self.w[(li, "sws")] = P.dram(f"sgu_w_s_{li}", [16, 128, 128], F32, kind="ExternalInput")
                self.w[(li, "sbs")] = P.dram(f"sgu_b_s_{li}", [16 * 128], F32, kind="ExternalInput")
            if kind == 0:
                self.w[(li, "pin")] = P.dram(f"pool_w_in_{li}", [D, D], F32, kind="ExternalInput")
                self.w[(li, "pgrp")] = P.dram(f"pool_w_grp_{li}", [4, 512, 512], F32, kind="ExternalInput")
                self.w[(li, "pout")] = P.dram(f"pool_w_out_{li}", [D, D], F32, kind="ExternalInput")
        self.A = P.sbuf("arenaA", [128, 22528], F32)
        self.B = P.sbuf("arenaB", [128, 8192], F32)
        self.wbufs = Rot([P.sbuf(f"wbuf{i}", [128, 6144], BF16) for i in range(3)])
        self.cst_sb = P.sbuf("cst_sb", [128, CST_N], F32)
        self.vec_sb = [P.sbuf(f"vec_sb{i}", [128, VC_N], F32) for i in range(2)]
        self.ones_bf = P.sbuf("ones_bf", [128, 128], BF16)
        self.eps_rms = P.sbuf("eps_rms", [128, 1], F32)
        self.ones_f32 = P.sbuf("ones_f32", [128, 128], F32)
        self.S = P.sbuf("arenaS", [128, 4096], F32)
        S = self.S
        self.abufs = Rot([S[:, i * 514:(i + 1) * 514] for i in range(3)])
        self.accs = Rot([S[:, 1544 + i * 512:1544 + (i + 1) * 512] for i in range(2)])
        self.sils = Rot([S[:, 2568 + i * 512:2568 + (i + 1) * 512] for i in range(2)])
        self.gtmp = Rot([S[:, i * 512:(i + 1) * 512] for i in range(6)])
        self.eps_ln = P.sbuf("eps_ln", [128, 1], F32)
        self.lnst = P.sbuf("lnst", [128, 8], F32)
        self.junk = P.sbuf("junk", [128, 2048], BF16)
        self.sqs = Rot([P.sbuf(f"sq{i}", [128, 512], BF16) for i in range(3)])
        self.sd = P.sbuf("sd", [128, 512], F32)
        self.rstd = P.sbuf("rstd", [128, 512], F32)
        self.hch = Rot([P.sbuf(f"hch{i}", [128, 512], F32) for i in range(3)])
        self.ahalo = P.sbuf("ahalo", [128, FC, 2], F32)
        self.zhalo = P.sbuf("zhalo", [128, DC, 16], F32)
        self.zbufs = Rot([S[:, i * 528:(i + 1) * 528] for i in range(3)])
        self.swbufs = Rot([S[:, 1584 + i * 528:1584 + (i + 1) * 528] for i in range(4)])
        self.tmp16 = P.sbuf("tmp16", [128, 16], F32)
        self.nsm = P.sbuf("nsa_sm", [128, 2304], F32)
        self.acc_ctr = 0
        self.nsmall_sb = P.sbuf("nsmall_sb", [128, 80], F32)
        self.banks = [P.psum(f"ps{i}", [128, 512], F32) for i in range(8)]
        self.mmb = Rot(self.banks[0:4])
        self.stat = self.banks[4]
        self.misc = Rot(self.banks[5:8])

    def build(self):
        P = self.P
        P.dma("sp", self.cst_sb[:], self.cst)
        P.memset(self.ones_bf[:], 1.0)
        P.memset(self.ones_f32[:], 1.0)
        P.memset(self.eps_rms[:], EPS_RMS)
        P.memset(self.eps_ln[:], EPS_LN)
        for it in range(NT):
            P.dma("sp", self.hT[it].rearrange("c p t -> p c t"),
                  self.xT[:, it * TT:(it + 1) * TT].rearrange("(c p) t -> p c t", p=128))
        nl = len(self.layers)
        for idx, li in enumerate(self.layers):
            vec = self.vec_sb[idx % 2]
            P.dma("sp", vec[:], self.w[(li, "vec")])
            self.vec = vec
            kind = li % 3
            parts = getattr(self, "parts", "all")
            self.mixer_to_out = (parts == "mixer")
            if parts in ("all", "mixer"):
                if kind == 0:
                    self.pool_mixer(li)
                elif kind == 1:
                    self.sgu_mixer(li)
                else:
                    P.dma("sp", self.nsmall_sb[:], self.w[(li, "nsmall")])
                    self.nsa_mixer(li)
            if parts in ("all", "ffn"):
                self.ffn(li, last=(idx == nl - 1))
        return P.emit()

    def h_load_tile(self, it, dst):
        self.P.dma("sp", dst, self.hT[it].rearrange("c p t -> p c t"))

    def rstd_from_stat(self, n_feat, eps_tile):
        P = self.P
        P.act(self.sd[:], self.stat[:], AF.Sqrt, bias=eps_tile[:], scale=1.0 / n_feat)
        P.op("dve", lambda e: e.reciprocal(out=self.rstd[:], in_=self.sd[:]),
             reads=[self.sd[:]], writes=[self.rstd[:]])

    def prenorm(self, hin, gcol, out):
        P = self.P
        for c in range(DC):
            sq = self.sqs.next()
            P.tt(sq[:], hin[:, c, :], hin[:, c, :], ALU.mult)
            P.mm(self.stat[:], self.ones_bf[:], sq[:], start=(c == 0), stop=(c == DC - 1))
        self.rstd_from_stat(D, self.eps_rms)
        for c in range(DC):
            P.stt(out[:, c, :], hin[:, c, :], self.vec[:, gcol + c:gcol + c + 1], self.rstd[:],
                  ALU.mult, ALU.mult)

    def f_evac(self, pf, fT, n):
        P = self.P
        P.act(fT[:, n, :], pf[:], AF.Copy)
        if n == 0:
            P.tt(self.sd[:], fT[:, n, :], fT[:, n, :], ALU.mult)
        else:
            sq = self.sqs.next()
            P.tt(sq[:], fT[:, n, :], fT[:, n, :], ALU.mult)
            P.tt(self.sd[:], self.sd[:], sq[:], ALU.add)

    def postnorm_residual(self, fT, gcol, it, to_out):
        P = self.P
        P.mm(self.stat[:], self.ones_f32[:], self.sd[:], start=True, stop=True)
        self.rstd_from_stat(D, self.eps_rms)
        for n in range(DC):
            hc = self.hch.next()
            P.dma("sp", hc[:], self.hT[it, n])
            P.stt(fT[:, n, :], fT[:, n, :], self.vec[:, gcol + n:gcol + n + 1], self.rstd[:],
                  ALU.mult, ALU.mult)
            P.tt(hc[:], hc[:], fT[:, n, :], ALU.add)
            if to_out:
                dst = self.outT[n * 128:(n + 1) * 128, it * TT:(it + 1) * TT]
                P.dma("sp", dst, hc[:], writes=[(dst, (it, n))])
            else:
                P.dma("sp", self.hT[it, n], hc[:])

    def slab(self, wdram, n0, ncols, kc):
        wb = self.wbufs.next()
        v = wb[:, 0:kc * ncols].rearrange("p (c n) -> p c n", n=ncols)
        self.P.dma("pool", v, wdram[:, n0:n0 + ncols].rearrange("(c p) n -> p c n", p=128))
        return v

    def ffn(self, li, last):
        P = self.P
        Abf = self.A[:].bitcast(BF16)
        gT = Abf[:, 0:FC * 1024].rearrange("p (c t) -> p c t", t=1024)
        hin = self.A[:, 0:8192].rearrange("p (c t) -> p c t", t=TT)
        uT = self.B[:].bitcast(BF16).rearrange("p (c t) -> p c t", t=1024)
        fT = self.B[:].rearrange("p (c t) -> p c t", t=TT)
        w_up, w_down = self.w[(li, "up")], self.w[(li, "down")]
        vec = self.vec
        for st in range(TL // 1024):
            for half in range(2):
                it = st * 2 + half
                self.h_load_tile(it, hin)
                self.prenorm(hin, VC_G2, uT[:, :, half * TT:(half + 1) * TT])
            for j in range(FC):
                wb = self.wbufs.next()
                wv = wb[:, 0:4096].rearrange("p (h c n) -> p h c n", h=2, n=128)
                P.dma("pool", wv[:, 0], w_up[:, j * 128:(j + 1) * 128].rearrange("(c p) n -> p c n", p=128))
                P.dma("pool", wv[:, 1], w_up[:, FF + j * 128:FF + (j + 1) * 128].rearrange("(c p) n -> p c n", p=128))
                prev_ab = None
                for half in range(2):
                    ts_ = slice(half * TT, (half + 1) * TT)
                    pa, pb = self.mmb.next(), self.mmb.next()
                    for k in range(DC):
                        P.mm(pa[:], wv[:, 0, k, :], uT[:, k, ts_], start=(k == 0), stop=(k == DC - 1))
                    for k in range(DC):
                        P.mm(pb[:], wv[:, 1, k, :], uT[:, k, ts_], start=(k == 0), stop=(k == DC - 1))
                    ab = self.abufs.next()
                    if half == 0:
                        if st == 0:
                            P.memset(ab[:, 0:2], 0.0)
                        else:
                            P.copy(ab[:, 0:2], self.ahalo[:, j, :])
                    else:
                        P.copy(ab[:, 0:2], prev_ab[:, 512:514])
                    P.act(ab[:, 2:514], pa[:], AF.Copy)
                    if half == 1 and st + 1 < TL // 1024:
                        P.copy(self.ahalo[:, j, :], ab[:, 512:514])
                    prev_ab = ab
                    acc = self.accs.next()
                    P.ts(acc[:], ab[:, 2:514], vec[:, VC_CW2 + j:VC_CW2 + j + 1], vec[:, VC_CB + j:VC_CB + j + 1],
                         op0=ALU.mult, op1=ALU.add)
                    P.stt(acc[:], ab[:, 1:513], vec[:, VC_CW1 + j:VC_CW1 + j + 1], acc[:], ALU.mult, ALU.add)
                    P.stt(acc[:], ab[:, 0:512], vec[:, VC_CW0 + j:VC_CW0 + j + 1], acc[:], ALU.mult, ALU.add)
                    sil = self.sils.next()
                    P.act(sil[:], acc[:], AF.Silu)
                    P.tt(gT[:, j, ts_], sil[:], pb[:], ALU.mult)
            for half in range(2):
                it = st * 2 + half
                ts_ = slice(half * TT, (half + 1) * TT)
                for n in range(DC):
                    wv = self.slab(w_down, n * 128, 128, FC)
                    pf = self.mmb.next()
                    for k in range(FC):
                        P.mm(pf[:], wv[:, k, :], gT[:, k, ts_], start=(k == 0), stop=(k == FC - 1))
                    self.f_evac(pf, fT, n)
                self.postnorm_residual(fT, VC_G3, it, to_out=last)

    def pool_mixer(self, li):
        P = self.P
        Abf = self.A[:].bitcast(BF16)
        hin = self.A[:, 0:8192].rearrange("p (c t) -> p c t", t=TT)
        pT = Abf[:, 16384:24576].rearrange("p (c t) -> p c t", t=TT)
        msT = Abf[:, 24576:32768].rearrange("p (c t) -> p c t", t=TT)
        uT = Abf[:, 32768:40960].rearrange("p (c t) -> p c t", t=TT)
        fT = self.B[:].rearrange("p (c t) -> p c t", t=TT)
        w_in, w_grp, w_out = self.w[(li, "pin")], self.w[(li, "pgrp")], self.w[(li, "pout")]
        vec = self.vec
        invc = self.cst_sb[:, CST_INVC:CST_INVC + 16]
        self.h_load_tile(0, hin)
        self.prenorm(hin, VC_G0, uT)
        for it in range(NT):
            for n in range(DC):
                wv = self.slab(w_in, n * 128, 128, DC)
                pz = self.mmb.next()
                for k in range(DC):
                    P.mm(pz[:], wv[:, k, :], uT[:, k, :], start=(k == 0), stop=(k == DC - 1))
                w = POOL_W[n // 4]
                zb = self.zbufs.next()
                if it == 0:
                    P.memset(zb[:, 0:16], 0.0)
                else:
                    P.copy(zb[:, 0:16], self.zhalo[:, n, :])
                P.act(zb[:, 16:528], pz[:], AF.Copy)
                if it + 1 < NT:
                    P.copy(self.zhalo[:, n, :], zb[:, 512:528])
                s = zb
                sh = 1
                while sh < w:
                    rem = w - 2 * sh
                    s2 = self.swbufs.next()
                    lo = 16 - rem
                    P.tt(s2[:, lo:528], s[:, lo:528], s[:, lo - sh:528 - sh], ALU.add)
                    s = s2
                    sh *= 2
                P.stt(pT[:, n, :], s[:, 16:528], 1.0 / w, zb[:, 16:528], ALU.mult, ALU.subtract)
                if it == 0:
                    P.tt(self.tmp16[:, 0:w - 1], s[:, 16:16 + w - 1], invc[:, 0:w - 1], ALU.mult)
                    P.tt(pT[:, n, 0:w - 1], self.tmp16[:, 0:w - 1], zb[:, 16:16 + w - 1], ALU.subtract)
            for g in range(4):
                for n2 in range(4):
                    wv = self.slab(w_grp[g], n2 * 128, 128, 4)
                    pm = self.mmb.next()
                    for k in range(4):
                        P.mm(pm[:], wv[:, k, :], pT[:, g * 4 + k, :], start=(k == 0), stop=(k == 3))
                    n = g * 4 + n2
                    P.act(msT[:, n, :], pm[:], AF.Copy, scale=vec[:, VC_X0 + n:VC_X0 + n + 1])
            if it + 1 < NT:
                self.h_load_tile(it + 1, hin)
                self.prenorm(hin, VC_G0, uT)
            for n in range(DC):
                wv = self.slab(w_out, n * 128, 128, DC)
                pf = self.mmb.next()
                for k in range(DC):
                    P.mm(pf[:], wv[:, k, :], msT[:, k, :], start=(k == 0), stop=(k == DC - 1))
                self.f_evac(pf, fT, n)
            self.postnorm_residual(fT, VC_G1, it, to_out=self.mixer_to_out)


    def gelu(self, dst, src, n):
        self.P.act(dst, src, AF.Gelu_apprx_tanh)

    def sgu_mixer(self, li):
        P = self.P
        Abf = self.A[:].bitcast(BF16)
        hin = self.A[:, 0:8192].rearrange("p (c t) -> p c t", t=TT)
        uT = Abf[:, 16384:24576].rearrange("p (c t) -> p c t", t=TT)
        uG = Abf[:, 24576:32768].rearrange("p (c t) -> p c t", t=TT)
        vn = Abf[:, 32768:40960].rearrange("p (b f) -> p b f", f=D)
        gatedT = uG
        Cb = self.A[:, 20480:22528].rearrange("p (g t) -> p g t", t=128)
        wsT = self.S[:, 3072:4096].bitcast(BF16).rearrange("p (g t) -> p g t", t=128)
        vtok = self.B[:].rearrange("p (b f) -> p b f", f=D)
        fT = self.B[:].rearrange("p (c t) -> p c t", t=TT)
        w_in, w_out = self.w[(li, "sin")], self.w[(li, "sout")]
        vec = self.vec
        ident = self.cst_sb[:, CST_IDENT:CST_IDENT + 128]
        tril = self.cst_sb[:, CST_TRIL:CST_TRIL + 128]
        wst = self.B[:, 0:2048].rearrange("p (g s) -> p g s", s=128)
        P.dma("sp", wst, self.w[(li, "sws")].rearrange("g t s -> t g s"))
        bsb = self.B[:, 2048:4096].rearrange("p (g t) -> p g t", t=128)
        P.dma("sp", bsb, self.w[(li, "sbs")].partition_broadcast(128).rearrange("p (g t) -> p g t", t=128))
        for g in range(16):
            P.tt(wst[:, g, :], wst[:, g, :], tril, ALU.mult)
            pt = self.misc.next()
            P.transpose(pt[:, 0:128], wst[:, g, :], ident)
            P.copy(wsT[:, g, :], pt[:, 0:128])
        for g in range(16):
            pr = self.misc.next()
            P.mm(pr[:, 0:128], self.ones_bf[:], wsT[:, g, :], start=True, stop=True)
            P.stt(Cb[:, g, :], pr[:, 0:128], vec[:, VC_X0 + 16 + g:VC_X0 + 17 + g], bsb[:, g, :], ALU.mult, ALU.add)
        self.h_load_tile(0, hin)
        self.prenorm(hin, VC_G0, uT)
        for it in range(NT):
            for n in range(DC):
                wv = self.slab(w_in, n * 128, 128, DC)
                pu = self.mmb.next()
                for k in range(DC):
                    P.mm(pu[:], wv[:, k, :], uT[:, k, :], start=(k == 0), stop=(k == DC - 1))
                self.gelu(uG[:, n, :], pu[:], TT)
            for fs in range(8):
                wv = self.slab(w_in, D + fs * 256, 256, DC)
                for tb in range(4):
                    pv = self.mmb.next()
                    for k in range(DC):
                        P.mm(pv[:, 0:256], uT[:, k, tb * 128:(tb + 1) * 128], wv[:, k, :],
                             start=(k == 0), stop=(k == DC - 1))
                    self.gelu(vtok[:, tb, fs * 256:(fs + 1) * 256], pv[:, 0:256], 256)
            st = self.lnst
            for tb in range(4):
                P.act(self.junk[:], vtok[:, tb, :], AF.Copy, accum_out=st[:, 0:1])
                P.act(self.junk[:], vtok[:, tb, :], AF.Square, accum_out=st[:, 1:2])
                P.ts(st[:, 2:3], st[:, 0:1], 1.0 / D, None, op0=ALU.mult)
                P.tt(st[:, 3:4], st[:, 2:3], st[:, 2:3], ALU.mult)
                P.stt(st[:, 4:5], st[:, 1:2], 1.0 / D, st[:, 3:4], ALU.mult, ALU.subtract)
                P.act(st[:, 5:6], st[:, 4:5], AF.Sqrt, bias=self.eps_ln[:], scale=1.0)
                P.op("dve", lambda e, st=st: e.reciprocal(out=st[:, 6:7], in_=st[:, 5:6]),
                     reads=[st[:, 5:6]], writes=[st[:, 6:7]])
                P.stt(st[:, 7:8], st[:, 2:3], -1.0, st[:, 6:7], ALU.mult, ALU.mult)
                P.ts(vn[:, tb, :], vtok[:, tb, :], st[:, 6:7], st[:, 7:8], op0=ALU.mult, op1=ALU.add)
            for g in range(16):
                pm = self.mmb.next()
                for tb in range(4):
                    P.mm(pm[:, tb * 128:(tb + 1) * 128], vn[:, tb, g * 128:(g + 1) * 128], wsT[:, g, :],
                         start=True, stop=True)
                tmp = self.gtmp.next()
                cb4 = Cb[:, g:g + 1, :].broadcast_to([128, 4, 128])
                P.stt(tmp.rearrange("p (b t) -> p b t", t=128), pm[:].rearrange("p (b t) -> p b t", t=128),
                      vec[:, VC_X0 + g:VC_X0 + g + 1], cb4, ALU.mult, ALU.add)
                P.tt(gatedT[:, g, :], tmp, uG[:, g, :], ALU.mult)
            if it + 1 < NT:
                self.h_load_tile(it + 1, hin)
                self.prenorm(hin, VC_G0, uT)
            for n in range(DC):
                wv = self.slab(w_out, n * 128, 128, DC)
                pf = self.mmb.next()
                for k in range(DC):
                    P.mm(pf[:], wv[:, k, :], gatedT[:, k, :], start=(k == 0), stop=(k == DC - 1))
                self.f_evac(pf, fT, n)
            self.postnorm_residual(fT, VC_G1, it, to_out=self.mixer_to_out)

    def nsa_views(self):
        Abf = self.A[:].bitcast(BF16)
        v = {}
        v["ksT"] = Abf[:, 0:8192].rearrange("p (g t) -> p g t", t=TL)
        v["kwT"] = Abf[:, 8192:16384].rearrange("p (g t) -> p g t", t=TL)
        v["Vs"] = Abf[:, 16384:24640].rearrange("p (j g d) -> p j g d", g=4, d=129)
        v["Vw"] = Abf[:, 24640:32896].rearrange("p (j g d) -> p j g d", g=4, d=129)
        v["kcmpT"] = Abf[:, 32896:33408].rearrange("p (g c) -> p g c", c=128)
        v["vcx"] = Abf[:, 33408:34056].rearrange("p (g d) -> p g d", d=162)
        v["hidV"] = Abf[:, 34080:35104].rearrange("p (h g c) -> p h g c", h=2, c=128)
        v["gates"] = self.A[:, 17552:18320].rearrange("p (j c) -> p j c", c=48)
        v["uT"] = Abf[:, 36640:44832].rearrange("p (c t) -> p c t", t=TT)
        Bbf = self.B[:].bitcast(BF16)
        v["hin"] = self.B[:].rearrange("p (c t) -> p c t", t=TT)
        v["fT"] = v["hin"]
        v["qT"] = Bbf[:, 0:8192].rearrange("p (c t) -> p c t", t=TT)
        v["kc_t"] = Bbf[:, 0:2112].rearrange("p (g t) -> p g t", t=528)
        v["vc_t"] = Bbf[:, 2112:4224].rearrange("p (g t) -> p g t", t=528)
        v["hidK"] = Bbf[:, 4224:4480].rearrange("p (h c) -> p h c", c=128)
        v["ptile"] = Rot([Bbf[:, 8192 + i * 512:8192 + (i + 1) * 512] for i in range(4)])
        v["o_acc"] = self.B[:, 5120:7168].rearrange("p (b f) -> p b f", f=512)
        v["o_bf"] = Bbf[:, 14336:16384].rearrange("p (b f) -> p b f", f=512)
        v["ropex"] = Rot([self.B[:, 4096 + i * 512:4096 + (i + 1) * 512] for i in range(2)])
        v["ropet"] = Rot([self.B[:, 5120 + i * 512:5120 + (i + 1) * 512] for i in range(2)])
        v["cs"] = self.B[:, 6144:7168].rearrange("p (a t) -> p a t", t=TT)
        Sbf = self.S[:].bitcast(BF16)
        v["masks8"] = Sbf[:, 0:4096].rearrange("p (o t) -> p o t", t=TT)
        v["Eall"] = Sbf[:, 4096:6144]
        v["selbT"] = Sbf[:, 6144:6656]
        v["impacc"] = self.S[:, 3328:3456].rearrange("p (b s) -> p b s", s=32)
        v["maskc"] = Sbf[:, 6912:7424]
        v["selc"] = self.S[:, 3712:4096].rearrange("p (b s) -> p b s", s=96)
        N = self.nsm
        Nbf = N[:].bitcast(BF16)
        v["w2k"] = Nbf[:, 0:256].rearrange("p (h d) -> p h d", d=128)
        v["w2v"] = Nbf[:, 256:512].rearrange("p (h d) -> p h d", d=128)
        v["peT"] = Nbf[:, 512:576].rearrange("p (k l) -> p k l", l=32)
        v["kch"] = Nbf[:, 576:640].rearrange("p (g t) -> p g t", t=16)
        v["vch"] = Nbf[:, 640:704].rearrange("p (g t) -> p g t", t=16)
        v["ident_bf"] = Nbf[:, 704:832]
        v["bias1"] = N[:, 416:420]
        v["b2v_b"] = N[:, 420:548]
        v["gateb_b"] = N[:, 548:596]
        v["sc"] = N[:, 596:724]
        v["selwork"] = N[:, 724:884]
        return v

    def rope_store(self, v, pp, dest, first_c=0):
        P = self.P
        x = v["ropex"].next()
        P.act(x, pp[:], AF.Copy)
        pr = self.mmb.next()
        P.mm(pr[:], self.cst_sb[:, CST_RT:CST_RT + 128], x, start=True, stop=True)
        t1 = v["ropet"].next()
        P.tt(t1, x, v["cs"][:, 0, :], ALU.mult)
        P.tt(x, pr[:], v["cs"][:, 1, :], ALU.mult)
        P.tt(dest, t1, x, ALU.add)

    def nsa_mixer(self, li):
        P = self.P
        v = self.nsa_views()
        self.v = v
        w_in, w_out = self.w[(li, "nin")], self.w[(li, "nout")]
        vec = self.vec
        small = self.w[(li, "nsmall")]
        SCALE = 128.0 ** -0.5
        NEGB = -30000.0
        P.dma("pool", v["masks8"], self.w[(li, "nmask8")].rearrange("p (o t) -> p o t", t=TT))
        P.dma("pool", v["Eall"][0:32, :], self.w[(li, "neall")])
        P.dma("pool", v["peT"], small[:, 0:64].rearrange("p (k l) -> p k l", l=32))
        P.dma("pool", v["w2k"], self.w[(li, "nw2")][0].rearrange("(c p) n -> p c n", p=128))
        P.dma("pool", v["w2v"], self.w[(li, "nw2")][1].rearrange("(c p) n -> p c n", p=128))
        P.dma("pool", v["ident_bf"], self.cst[:, CST_IDENT:CST_IDENT + 128])
        P.dma("sp", v["b2v_b"], self.w[(li, "nb2v")].partition_broadcast(128))
        P.dma("sp", v["gateb_b"], self.w[(li, "ngb")].partition_broadcast(128))
        P.dma("pool", v["vcx"][:, :, 128:161], self.w[(li, "novl")].rearrange("p (g d) -> p g d", d=33))
        for j in range(16):
            P.memset(v["Vs"][:, j, :, 128:129], 1.0)
            P.memset(v["Vw"][:, j, :, 128:129], 1.0)
        for kv in range(2):
            for hc in range(2):
                wv = self.slab_w1(li, kv, hc)
                pp = self.misc.next()
                for l in range(32):
                    P.mm(pp[:, 0:1], wv[:, l, :], v["peT"][:, kv, l:l + 1], start=(l == 0), stop=(l == 31))
                P.tt(v["bias1"][:, kv * 2 + hc:kv * 2 + hc + 1], pp[:, 0:1],
                     self.nsmall_sb[:, 64 + kv * 2 + hc:65 + kv * 2 + hc], ALU.add)
        for it in range(NT):
            self.h_load_tile(it, v["hin"])
            self.prenorm(v["hin"], VC_G0, v["uT"])
            P.dma("sp", v["cs"], self.w[(li, "nrope")][:, :, it * TT:(it + 1) * TT])
            for name, col0, rope in (("kc", 2048, True), ("vc", 2560, False), ("ks", 3072, True), ("kw", 4096, True)):
                for g in range(4):
                    wv = self.slab(w_in, col0 + g * 128, 128, DC)
                    pp = self.mmb.next()
                    for k in range(DC):
                        P.mm(pp[:], wv[:, k, :], v["uT"][:, k, :], start=(k == 0), stop=(k == DC - 1))
                    if name == "kc":
                        dest = v["kc_t"][:, g, 16:528]
                    elif name == "vc":
                        dest = v["vc_t"][:, g, 16:528]
                    elif name == "ks":
                        dest = v["ksT"][:, g, it * TT:(it + 1) * TT]
                    else:
                        dest = v["kwT"][:, g, it * TT:(it + 1) * TT]
                    if rope:
                        self.rope_store(v, pp, dest)
                    else:
                        P.act(dest, pp[:], AF.Copy)
            for src, halo in ((v["kc_t"], v["kch"]), (v["vc_t"], v["vch"])):
                if it == 0:
                    P.memset(src[:, :, 0:16], 0.0)
                else:
                    P.copy(src[:, :, 0:16], halo)
                if it + 1 < NT:
                    P.copy(halo, src[:, :, 512:528])
            for V, col0 in ((v["Vs"], 3584), (v["Vw"], 4608)):
                for half in range(2):
                    wv = self.slab(w_in, col0 + half * 256, 256, DC)
                    for tb in range(4):
                        pv = self.mmb.next()
                        for k in range(DC):
                            P.mm(pv[:, 0:256], v["uT"][:, k, tb * 128:(tb + 1) * 128], wv[:, k, :],
                                 start=(k == 0), stop=(k == DC - 1))
                        P.act(V[:, it * 4 + tb, half * 2:half * 2 + 2, 0:128],
                              pv[:, 0:256].rearrange("p (g d) -> p g d", d=128), AF.Copy)
            wv = self.slab(w_in, 5120, 48, DC)
            for tb in range(4):
                pg = self.misc.next()
                for k in range(DC):
                    P.mm(pg[:, 0:48], v["uT"][:, k, tb * 128:(tb + 1) * 128], wv[:, k, :],
                         start=(k == 0), stop=(k == DC - 1))
                P.tt(v["sc"][:, 0:48], pg[:, 0:48], v["gateb_b"], ALU.add)
                P.act(v["gates"][:, it * 4 + tb, :], v["sc"][:, 0:48], AF.Sigmoid)
            j0 = 1 if it == 0 else 0
            nj = 32 - j0
            c0 = 32 * it - 1 + j0
            if it == 0:
                P.memset(v["hidK"], 0.0)
            for kv, src in ((0, v["kc_t"]), (1, v["vc_t"])):
                for hc in range(2):
                    wv = self.slab_w1(li, kv, hc)
                    ph = self.mmb.next()
                    for l in range(32):
                        rhs = src[:, :, l:l + 497:16]
                        P.mm(ph[:, 0:128].rearrange("p (g j) -> p g j", j=32), wv[:, l, :], rhs,
                             start=(l == 0), stop=(l == 31))
                    phv = ph[:, 0:128].rearrange("p (g j) -> p g j", j=32)[:, :, j0:32]
                    if kv == 0:
                        dst = v["hidK"][:, hc, :].rearrange("p (g j) -> p g j", j=32)[:, :, j0:32]
                    else:
                        dst = v["hidV"][:, hc, :, c0:c0 + nj]
                    self.gelu3(dst, phv, v["bias1"][:, kv * 2 + hc:kv * 2 + hc + 1], nj)
                if kv == 0:
                    pk = self.mmb.next()
                    for hc in range(2):
                        P.mm(pk[:, 0:128], v["w2k"][:, hc, :], v["hidK"][:, hc, :], start=(hc == 0), stop=(hc == 1))
                    P.act(v["kcmpT"][:, :, c0:c0 + nj],
                          pk[:, 0:128].rearrange("p (g j) -> p g j", j=32)[:, :, j0:32],
                          AF.Identity, bias=self.nsmall_sb[:, 68:69], scale=1.0)
        for g in range(4):
            pv = self.misc.next()
            for hc in range(2):
                P.mm(pv[0:127, 0:128], v["hidV"][:, hc, g, 0:127], v["w2v"][:, hc, :], start=(hc == 0), stop=(hc == 1))
            P.tt(v["vcx"][0:127, g, 0:128], pv[0:127, 0:128], v["b2v_b"][0:127, :], ALU.add)
        accX, accY = self.banks[5], self.banks[6]
        misc2 = Rot([self.banks[7], self.banks[4]])
        for it in range(NT):
            self.h_load_tile(it, v["hin"])
            self.prenorm(v["hin"], VC_G0, v["uT"])
            P.dma("sp", v["cs"], self.w[(li, "nrope")][:, :, it * TT:(it + 1) * TT])
            for h in range(16):
                wv = self.slab(w_in, h * 128, 128, DC)
                pq = self.mmb.next()
                for k in range(DC):
                    P.mm(pq[:], wv[:, k, :], v["uT"][:, k, :], start=(k == 0), stop=(k == DC - 1))
                self.rope_store(v, pq, v["qT"][:, h, :])
            P.dma("pool", v["maskc"], self.w[(li, "nmaskc")][:, it * TT:(it + 1) * TT])
            P.dma("sp", v["selc"], self.w[(li, "nselc")][:, it * 4:(it + 1) * 4, :])
            oT = v["uT"]
            for g in range(4):
                self.nsa_group(li, it, g, v, accX, accY, misc2, SCALE, oT)
            fT = v["fT"]
            for n in range(DC):
                wv = self.slab(w_out, n * 128, 128, DC)
                pf = self.mmb.next()
                for k in range(DC):
                    P.mm(pf[:], wv[:, k, :], oT[:, k, :], start=(k == 0), stop=(k == DC - 1))
                self.f_evac(pf, fT, n)
            self.postnorm_residual(fT, VC_G1, it, to_out=self.mixer_to_out)

    def slab_w1(self, li, kv, hc):
        wb = self.wbufs.next()
        vv = wb[:, 0:4096].rearrange("p (l n) -> p l n", n=128)
        self.P.dma("pool", vv, self.w[(li, "nw1")][kv][:, hc * 128:(hc + 1) * 128].rearrange("(l d) n -> d l n", d=128))
        return vv

    def gelu3(self, dst, src, bias, nj):
        self.P.act(dst, src, AF.Gelu_apprx_tanh, bias=bias, scale=1.0)

    def nsa_group(self, li, it, g, v, accX, accY, misc2, SCALE, oT):
        P = self.P
        qT, gates = v["qT"], v["gates"]
        ident_bf = v["ident_bf"]
        o_acc, o_bf = v["o_acc"], v["o_bf"]
        N = self.nsm
        t4s = Rot([N[:, 1024 + 8 * i:1032 + 8 * i] for i in range(8)])
        cmb = Rot([N[:, 1088 + 256 * i:1344 + 256 * i].rearrange("p (b d) -> p b d", d=128) for i in range(2)])
        tmpi = N[:, 1600:1728].rearrange("p (b s) -> p b s", s=32)
        imp2 = N[:, 1728:1856].rearrange("p (b s) -> p b s", s=32)
        imp3 = N[:, 1856:1984].rearrange("p (b s) -> p b s", s=32)
        selb = N[:, 1984:2112].rearrange("p (b s) -> p b s", s=32)
        m8 = N[:, 2112:2144].rearrange("p (b k) -> p b k", k=8)
        m8b = N[:, 2144:2176].rearrange("p (b k) -> p b k", k=8)
        selbias = N[:, 2176:2240].bitcast(BF16).rearrange("p (b s) -> p b s", s=32)
        gq = gates[:, it * 4:(it + 1) * 4, :]
        def cmp_A(r):
            h = 4 * g + r
            psc = self.mmb.next()
            P.mm(psc[0:127, :], v["kcmpT"][:, g, 0:127], qT[:, h, :], start=True, stop=False)
            P.mm(psc[0:127, :], ident_bf[0:127, 0:127], v["maskc"][0:127, :], start=False, stop=True)
            pcT = v["ptile"].next()
            P.act(pcT[0:127, :], psc[0:127, :], AF.Exp, scale=SCALE)
            return pcT

        def cmp_B(r, pcT):
            h = 4 * g + r
            pocA = self.mmb.next()
            pocB = self.mmb.next()
            for tb in range(4):
                P.mm(pocA[:, tb * 128:(tb + 1) * 128], pcT[0:127, tb * 128:(tb + 1) * 128], v["vcx"][0:127, g, 0:128],
                     start=(tb == 0), stop=False, skip_group_check=True)
            for tb in range(4):
                P.mm(pocB[:, tb * 33:(tb + 1) * 33], pcT[0:127, tb * 128:(tb + 1) * 128], v["vcx"][0:127, g, 128:161],
                     start=(tb == 0), stop=False, skip_group_check=True)
            pBv = pocB[:, 0:132].rearrange("p (b s) -> p b s", s=33)
            t4 = t4s.next()
            rs4 = t4[:, 0:4]
            P.ts(rs4, pBv[:, :, 0], 1e-30, None, op0=ALU.add)
            P.op("dve", lambda e, rs4=rs4: e.reciprocal(out=rs4, in_=rs4), reads=[rs4], writes=[rs4])
            rsb = rs4.unsqueeze(2).broadcast_to([128, 4, 32])
            if r == 0:
                P.tt(v["impacc"], pBv[:, :, 1:33], rsb, ALU.mult)
            else:
                P.tt(tmpi, pBv[:, :, 1:33], rsb, ALU.mult)
                P.tt(v["impacc"], v["impacc"], tmpi, ALU.add)
            wc4 = t4[:, 4:8]
            P.tt(wc4, rs4, gq[:, :, h * 3], ALU.mult)
            P.tt(o_acc[:, :, r * 128:(r + 1) * 128], pocA[:].rearrange("p (b d) -> p b d", d=128),
                 wc4.unsqueeze(2).broadcast_to([128, 4, 128]), ALU.mult)

        pc_cur = cmp_A(0)
        for r in range(4):
            pc_next = cmp_A(r + 1) if r < 3 else None
            cmp_B(r, pc_cur)
            pc_cur = pc_next
        P.tt(imp2, v["impacc"], v["selc"][:, :, 0:32], ALU.mult)
        P.tt(imp2, imp2, v["selc"][:, :, 32:64], ALU.add)
        for tb in range(4):
            a2, a3, b8, c8 = imp2[:, tb, :], imp3[:, tb, :], m8[:, tb, :], m8b[:, tb, :]
            P.op("dve", lambda e, b8=b8, a2=a2: e.max(out=b8, in_=a2), reads=[a2], writes=[b8])
            P.op("dve", lambda e, b8=b8, a2=a2, a3=a3: e.match_replace(out=a3, in_to_replace=b8, in_values=a2, imm_value=-1e9),
                 reads=[a2, b8], writes=[a3])
            P.op("dve", lambda e, c8=c8, a3=a3: e.max(out=c8, in_=a3), reads=[a3], writes=[c8])
        P.tt(selb, imp2, m8b[:, :, 7:8].broadcast_to([128, 4, 32]), ALU.is_ge)
        P.tt(selb, selb, v["selc"][:, :, 64:96], ALU.mult)
        P.ts(selbias, selb, 30000.0, -30000.0, op0=ALU.mult, op1=ALU.add)
        pt = self.mmb.next()
        ptb = pt[:].bitcast(BF16)
        for tb in range(4):
            P.transpose(ptb[0:32, tb * 128:(tb + 1) * 128], selbias[:, tb, :], ident_bf)
        P.copy(v["selbT"][0:32, :], ptb[0:32, 0:512])
        for br in range(2):
            KT = v["ksT"] if br == 0 else v["kwT"]
            V = v["Vs"] if br == 0 else v["Vw"]
            kt_lo = 0 if br == 0 else max(0, 4 * it - 4)
            for r in range(4):
                h = 4 * g + r
                self.acc_ctr += 1
                bX, bY = (self.banks[5], self.banks[6]) if self.acc_ctr % 2 == 0 else (self.banks[7], self.banks[4])
                started = {0: False, 1: False}
                last_kt = 4 * it + 3
                def stA(kt):
                    o = kt - 4 * it
                    if br == 0:
                        c0, c1 = max(o, 0) * 128, 512
                        tbs = list(range(max(o, 0), 4))
                    else:
                        tbs = [tb for tb in range(4) if 0 <= tb - o <= 4]
                        c0, c1 = tbs[0] * 128, (tbs[-1] + 1) * 128
                    pss = self.mmb.next()
                    need_mask = (o >= 0) if br == 0 else True
                    P.mm(pss[:, c0:c1], KT[:, g, kt * 128:(kt + 1) * 128], qT[:, h, c0:c1], start=True, stop=False)
                    if br == 0:
                        P.mm(pss[:, c0:c1], v["Eall"][0:32, kt * 128:(kt + 1) * 128], v["selbT"][0:32, c0:c1],
                             start=False, stop=(not need_mask))
                    if need_mask:
                        P.mm(pss[:, c0:c1], ident_bf, v["masks8"][:, o + 4, c0:c1], start=False, stop=True)
                    pT = v["ptile"].next()
                    P.act(pT[:, c0:c1], pss[:, c0:c1], AF.Exp, scale=SCALE)
                    return pT, tbs

                def stB(kt, pT, tbs):
                    for tb in tbs:
                        bank, bi = (bX, 0) if tb < 2 else (bY, 1)
                        av = bank[:, (tb % 2) * 129:(tb % 2) * 129 + 129]
                        P.mm(av, pT[:, tb * 128:(tb + 1) * 128], V[:, kt, g, 0:129],
                             start=(not started[bi]), stop=False, skip_group_check=True)
                        started[bi] = True

                kts = list(range(kt_lo, last_kt + 1))
                cur = stA(kts[0])
                for ki, kt in enumerate(kts):
                    nxt = stA(kts[ki + 1]) if ki + 1 < len(kts) else None
                    stB(kt, *cur)
                    cur = nxt
                t4 = t4s.next()
                rs4, w4 = t4[:, 0:4], t4[:, 4:8]
                for bi, bank in ((0, bX), (1, bY)):
                    bv = bank[:, 0:258].rearrange("p (b d) -> p b d", d=129)
                    P.op("dve", lambda e, o_=rs4[:, 2 * bi:2 * bi + 2], i_=bv[:, :, 128]: e.reciprocal(out=o_, in_=i_),
                         reads=[bv[:, :, 128]], writes=[rs4[:, 2 * bi:2 * bi + 2]])
                P.tt(w4, rs4, gq[:, :, h * 3 + 1 + br], ALU.mult)
                for bi, bank in ((0, bX), (1, bY)):
                    bv = bank[:, 0:258].rearrange("p (b d) -> p b d", d=129)
                    tmp = cmb.next()
                    P.tt(tmp, bv[:, :, 0:128], w4[:, 2 * bi:2 * bi + 2].unsqueeze(2).broadcast_to([128, 2, 128]), ALU.mult)
                    src = o_acc[:, 2 * bi:2 * bi + 2, r * 128:(r + 1) * 128]
                    if br == 0:
                        P.tt(src, src, tmp, ALU.add)
                    else:
                        P.tt(o_bf[:, 2 * bi:2 * bi + 2, r * 128:(r + 1) * 128], src, tmp, ALU.add)
        for tb in range(4):
            pt = self.mmb.next()
            ptb = pt[:].bitcast(BF16)
            for r in range(4):
                P.transpose(ptb[:, r * 128:(r + 1) * 128], o_bf[:, tb, r * 128:(r + 1) * 128], ident_bf)
            P.copy(oT[:, 4 * g:4 * g + 4, tb * 128:(tb + 1) * 128],
                   ptb[:, 0:512].rearrange("p (r t) -> p r t", t=128))


CST_INVC = 0
CST_IDENT = 16
CST_TRIL = 144
CST_RT = 272
CST_N = 400


def make_consts():
    c = np.zeros((128, CST_N), np.float32)
    c[:, CST_INVC:CST_INVC + 16] = (1.0 / np.arange(1, 17, dtype=np.float32))[None, :]
    c[:, CST_IDENT:CST_IDENT + 128] = np.eye(128, dtype=np.float32)
    c[:, CST_TRIL:CST_TRIL + 128] = np.tril(np.ones((128, 128), np.float32))
    R = np.zeros((128, 128), np.float32)
    for mm_ in range(64):
        R[mm_, mm_ + 64] = -1.0
        R[mm_ + 64, mm_] = 1.0
    c[:, CST_RT:CST_RT + 128] = R.T
    return c


def pack_cols(v):
    return np.ascontiguousarray(np.asarray(v, np.float32).reshape(-1, 128).T)


def make_inputs(inputs, layers, b):
    m = {"xT": np.ascontiguousarray(inputs["x"][b].T), "cst": make_consts()}
    for li in layers:
        kind, j = li % 3, li // 3
        vec = np.zeros((128, VC_N), np.float32)
        for q in range(4):
            vec[:, q * 16:(q + 1) * 16] = pack_cols(inputs["norm_g"][li, q])
        for q in range(3):
            vec[:, VC_CW0 + q * 44:VC_CW0 + (q + 1) * 44] = pack_cols(inputs["ffn_conv_w"][li, q])
        vec[:, VC_CB:VC_CB + 44] = pack_cols(inputs["ffn_conv_b"][li])
        m[f"ffn_w_up_{li}"] = inputs["ffn_w_up"][li]
        m[f"ffn_w_down_{li}"] = inputs["ffn_w_down"][li]
        if kind == 2:
            m[f"nsa_w_in_{li}"] = inputs["nsa_w_in"][j]
            m[f"nsa_w_out_{li}"] = inputs["nsa_w_out"][j]
            m[f"nsa_cmp_w1_{li}"] = inputs["nsa_cmp_w1"][j]
            m[f"nsa_cmp_w2_{li}"] = inputs["nsa_cmp_w2"][j]
            sm = np.zeros((128, 80), np.float32)
            sm[:, 0:32] = inputs["nsa_cmp_pe"][j, 0].T
            sm[:, 32:64] = inputs["nsa_cmp_pe"][j, 1].T
            for kv in range(2):
                sm[:, 64 + kv * 2:66 + kv * 2] = pack_cols(inputs["nsa_cmp_b1"][j, kv])
            sm[:, 68] = inputs["nsa_cmp_b2"][j, 0]
            m[f"nsa_small_{li}"] = sm
            m[f"nsa_b2v_{li}"] = np.ascontiguousarray(inputs["nsa_cmp_b2"][j, 1])
            m[f"nsa_gate_b_{li}"] = np.ascontiguousarray(inputs["nsa_gate_b"][j])
            m.update({f"{k}_{li}": val for k, val in nsa_tables().items()})
        if kind == 1:
            vec[:, VC_X0:VC_X0 + 16] = pack_cols(inputs["sgu_ln_g"][j])
            vec[:, VC_X0 + 16:VC_X0 + 32] = pack_cols(inputs["sgu_ln_b"][j])
            m[f"sgu_w_in_{li}"] = inputs["sgu_w_in"][j]
            m[f"sgu_w_out_{li}"] = inputs["sgu_w_out"][j]
            m[f"sgu_w_s_{li}"] = inputs["sgu_w_s"][j]
            m[f"sgu_b_s_{li}"] = np.ascontiguousarray(inputs["sgu_b_s"][j].reshape(-1))
        if kind == 0:
            vec[:, VC_X0:VC_X0 + 16] = pack_cols(inputs["pool_scale"][j])
            m[f"pool_w_in_{li}"] = inputs["pool_w_in"][j]
            m[f"pool_w_grp_{li}"] = inputs["pool_w_grp"][j]
            m[f"pool_w_out_{li}"] = inputs["pool_w_out"][j]
        m[f"vec_{li}"] = vec
    return m


_NSA_TABLES = None


def nsa_tables():
    global _NSA_TABLES
    if _NSA_TABLES is not None:
        return _NSA_TABLES
    NEGB = -30000.0
    s = np.arange(128)[:, None]
    t = np.arange(512)[None, :]
    m8 = np.zeros((128, 8, 512), np.float32)
    for o in range(-4, 4):
        if o >= 0:
            ok = (o * 128 + s) <= t
        else:
            ok = (t - s) < (512 + o * 128)
        m8[:, o + 4, :] = np.where(ok, 0.0, NEGB)
    eall = (np.arange(2048)[None, :] // 64 == np.arange(32)[:, None]).astype(np.float32)
    c = np.arange(128)[:, None]
    sj = np.arange(32)[None, :]
    ovl = ((c * 16 <= (sj + 1) * 64 - 1) & (c * 16 + 31 >= sj * 64)).astype(np.float32)
    ovl33 = np.concatenate([np.ones((128, 1), np.float32), ovl], axis=1)
    ovl_all = np.tile(ovl33[:, None, :], (1, 4, 1)).reshape(128, 132)
    half = 64
    inv = 1.0 / (10000.0 ** (np.arange(half, dtype=np.float32) / half))
    ang = np.arange(TL, dtype=np.float32)[None, :] * np.concatenate([inv, inv])[:, None].astype(np.float32)
    rope = np.stack([np.cos(ang), np.sin(ang)], axis=1).astype(np.float32)
    tt = np.arange(TL)[None, :]
    maskc = np.where((c * 16 + 31) <= tt, 0.0, NEGB).astype(np.float32)
    tpos = np.arange(TL)
    cur = (tpos // 64)[:, None]
    blk = np.arange(32)[None, :]
    forced = (blk == 0) | (blk == cur) | (blk == cur - 1)
    valid = blk <= cur
    keep = (valid & ~forced).astype(np.float32)
    add = np.where(forced, 10.0, 0.0) + np.where(valid, 0.0, -10.0)
    selc = np.concatenate([keep, add.astype(np.float32), valid.astype(np.float32)], axis=1)
    selc = np.ascontiguousarray(selc.reshape(16, 128, 96).transpose(1, 0, 2))
    _NSA_TABLES = {"nsa_mask8": np.ascontiguousarray(m8.reshape(128, 4096)), "nsa_eall": eall,
                   "nsa_ovl": ovl_all, "nsa_rope": rope, "nsa_maskc": maskc, "nsa_selc": selc}
    return _NSA_TABLES


from concourse.bass_utils import run_bass_kernel_spmd

N_SEQ_CORES = 4


def kernel(**inputs):
    layers = [0, 1, 2, 3]
    bld = Builder(layers)
    nc = bld.build()
    in_maps = [make_inputs(inputs, layers, b) for b in range(N_SEQ_CORES)]
    res = run_bass_kernel_spmd(nc, in_maps, core_ids=list(range(N_SEQ_CORES)))
    out = np.stack([np.asarray(res.results[b]["outT"]).T for b in range(N_SEQ_CORES)])
    return np.ascontiguousarray(out.astype(np.float32))
```
